# Optimizing a Trainium2 kernel written in Bass

```python
import jax, jax.numpy as jnp
from jax import lax
import numpy as np

D_MODEL = 2048
BATCH = 4
SEQ = 8192
DEPTH = 2

GRID_W = 64
HEAD_DIM = 128
D_MIX = D_MODEL
FOURIER_GROUPS = D_MIX // 4 // HEAD_DIM
FOURIER_WIDTH = FOURIER_GROUPS * HEAD_DIM
MEM_HEADS = 4
MEM_WIDTH = MEM_HEADS * HEAD_DIM
NA_WIDTH = D_MIX - FOURIER_WIDTH - MEM_WIDTH
NA_HEADS = NA_WIDTH // HEAD_DIM
NA_KH = 8
NA_KW = 16
N_MEM = 256
D_IN = 3 * NA_WIDTH + FOURIER_WIDTH + MEM_WIDTH
D_FF = ((8 * D_MODEL // 3 + 255) // 256) * 256
EPS = 1e-6

kernel_name = "hybrid_natten_fnet_memory_encoder"


def rms_norm(x, g):
    xf = x.astype(jnp.float32)
    y = xf * lax.rsqrt(jnp.mean(xf * xf, axis=-1, keepdims=True) + EPS)
    return (y * g.astype(jnp.float32)).astype(x.dtype)


def neighbourhood_attention(q, k, v, rpb):
    B, S, H, Dh = q.shape
    rows = S // GRID_W
    kh = min(NA_KH, rows)
    kw = min(NA_KW, GRID_W)
    scale = Dh ** -0.5
    qg = q.reshape(B, rows, GRID_W, H, Dh)
    kg = k.reshape(B, rows, GRID_W, H, Dh)
    vg = v.reshape(B, rows, GRID_W, H, Dh)

    col = jnp.arange(GRID_W)
    col_start = jnp.clip(col - kw // 2, 0, GRID_W - kw)
    col_mask = (col[None, :] >= col_start[:, None]) & (col[None, :] < col_start[:, None] + kw)
    col_idx = jnp.clip(col[None, :] - col[:, None] + NA_KW - 1, 0, 2 * NA_KW - 2)
    bias_col = rpb[:, :, col_idx]

    def one_row(r):
        start = jnp.clip(r - kh // 2, 0, rows - kh)
        q_r = lax.dynamic_index_in_dim(qg, r, axis=1, keepdims=False)
        k_b = lax.dynamic_slice_in_dim(kg, start, kh, axis=1)
        v_b = lax.dynamic_slice_in_dim(vg, start, kh, axis=1)
        s = jnp.einsum('bqhd,bakhd->bhqak', q_r, k_b,
                       preferred_element_type=jnp.float32) * scale
        row_idx = start + jnp.arange(kh) - r + NA_KH - 1
        bias = jnp.take(bias_col, row_idx, axis=1).transpose(0, 2, 1, 3)
        s = s + bias[None].astype(jnp.float32)
        s = jnp.where(col_mask[None, None, :, None, :], s, -jnp.inf)
        p = jax.nn.softmax(s.reshape(B, H, GRID_W, kh * GRID_W), axis=-1)
        p = p.reshape(B, H, GRID_W, kh, GRID_W).astype(v.dtype)
        return jnp.einsum('bhqak,bakhd->bqhd', p, v_b)

    out = lax.map(one_row, jnp.arange(rows))
    return out.transpose(1, 0, 2, 3, 4).reshape(B, S, H * Dh)


def fourier_mix(u, w_f):
    B, S, _ = u.shape
    uf = u.astype(jnp.float32).reshape(B, S, FOURIER_GROUPS, HEAD_DIM)
    y = jnp.fft.fft2(uf, axes=(1, 3), norm="ortho").real
    y = jnp.einsum('bsgc,gce->bsge', y, w_f.astype(jnp.float32))
    return y.reshape(B, S, FOURIER_WIDTH).astype(u.dtype)


def memory_attention(q, mk, mv):
    B, S, H, Dh = q.shape
    s = jnp.einsum('bshd,bmhd->bhsm', q, mk, preferred_element_type=jnp.float32) * (Dh ** -0.5)
    p = jax.nn.softmax(s, axis=-1).astype(mv.dtype)
    return jnp.einsum('bhsm,bmhd->bshd', p, mv).reshape(B, S, H * Dh)


def setup_inputs(seed: int = 0) -> dict:
    key = jax.random.key(seed)
    ks = jax.random.split(key, 20)
    f32 = jnp.float32

    def nrm(k, shape, scale):
        return jax.random.normal(k, shape, f32) * scale

    def gain(k, shape):
        return 1.0 + 0.05 * jax.random.normal(k, shape, f32)

    return {
        "x": jax.random.normal(ks[0], (BATCH, SEQ, D_MODEL), f32),
        "mem": jax.random.normal(ks[1], (BATCH, N_MEM, D_MODEL), f32),
        "attn_norm": gain(ks[2], (DEPTH, D_MODEL)),
        "w_in": nrm(ks[3], (DEPTH, D_MODEL, D_IN), D_MODEL ** -0.5),
        "na_q_norm": gain(ks[4], (DEPTH, HEAD_DIM)),
        "na_k_norm": gain(ks[5], (DEPTH, HEAD_DIM)),
        "na_rpb": nrm(ks[6], (DEPTH, NA_HEADS, 2 * NA_KH - 1, 2 * NA_KW - 1), 0.5),
        "w_fourier": nrm(ks[7], (DEPTH, FOURIER_GROUPS, HEAD_DIM, HEAD_DIM), HEAD_DIM ** -0.5),
        "mem_norm": gain(ks[8], (DEPTH, D_MODEL)),
        "w_mem_kv": nrm(ks[9], (DEPTH, D_MODEL, 2 * MEM_WIDTH), D_MODEL ** -0.5),
        "mem_q_norm": gain(ks[10], (DEPTH, HEAD_DIM)),
        "mem_k_norm": gain(ks[11], (DEPTH, HEAD_DIM)),
        "out_norm": gain(ks[12], (DEPTH, D_MIX)),
        "w_out": nrm(ks[13], (DEPTH, D_MIX, D_MODEL), D_MIX ** -0.5 * 0.5),
        "ffn_norm": gain(ks[14], (DEPTH, D_MODEL)),
        "w_gate": nrm(ks[15], (DEPTH, D_MODEL, D_FF), D_MODEL ** -0.5),
        "w_up": nrm(ks[16], (DEPTH, D_MODEL, D_FF), D_MODEL ** -0.5),
        "w_down": nrm(ks[17], (DEPTH, D_FF, D_MODEL), D_FF ** -0.5 * 0.5),
    }


def reference(x, mem, attn_norm, w_in, na_q_norm, na_k_norm, na_rpb, w_fourier,
              mem_norm, w_mem_kv, mem_q_norm, mem_k_norm, out_norm, w_out,
              ffn_norm, w_gate, w_up, w_down):
    B, S, _ = x.shape
    M = mem.shape[1]
    for l in range(DEPTH):
        h = rms_norm(x, attn_norm[l])
        proj = h @ w_in[l]
        o1 = NA_WIDTH
        o2 = 2 * NA_WIDTH
        o3 = 3 * NA_WIDTH
        o4 = o3 + FOURIER_WIDTH
        q_na = rms_norm(proj[..., :o1].reshape(B, S, NA_HEADS, HEAD_DIM), na_q_norm[l])
        k_na = rms_norm(proj[..., o1:o2].reshape(B, S, NA_HEADS, HEAD_DIM), na_k_norm[l])
        v_na = proj[..., o2:o3].reshape(B, S, NA_HEADS, HEAD_DIM)
        u_f = proj[..., o3:o4]
        q_m = rms_norm(proj[..., o4:].reshape(B, S, MEM_HEADS, HEAD_DIM), mem_q_norm[l])

        mem_n = rms_norm(mem, mem_norm[l])
        kv_m = mem_n @ w_mem_kv[l]
        k_m = rms_norm(kv_m[..., :MEM_WIDTH].reshape(B, M, MEM_HEADS, HEAD_DIM), mem_k_norm[l])
        v_m = kv_m[..., MEM_WIDTH:].reshape(B, M, MEM_HEADS, HEAD_DIM)

        y_na = neighbourhood_attention(q_na, k_na, v_na, na_rpb[l])
        y_f = fourier_mix(u_f, w_fourier[l])
        y_m = memory_attention(q_m, k_m, v_m)

        y = jnp.concatenate([y_na, y_f, y_m], axis=-1)
        y = rms_norm(y.reshape(B, S, D_MIX // HEAD_DIM, HEAD_DIM),
                     jnp.ones((HEAD_DIM,), x.dtype)).reshape(B, S, D_MIX) * out_norm[l]
        x = x + y @ w_out[l]

        h = rms_norm(x, ffn_norm[l])
        x = x + (jax.nn.silu(h @ w_gate[l]) * (h @ w_up[l])) @ w_down[l]
    return x
```

```python
from contextlib import ExitStack

import numpy as np
import ml_dtypes

import concourse.bass as bass
import concourse.mybir as mybir
from concourse.bass_utils import run_bass_kernel_spmd

F32 = mybir.dt.float32
BF16 = mybir.dt.bfloat16
AF = mybir.ActivationFunctionType
ALU = mybir.AluOpType

S = 8192
D = 2048
DIN = 4096
DFF = 5632
NFF = DFF // 128
T = 512
EPS = 1e-6
NEG = -30000.0
ENGS = ("pe", "act", "dve", "pool", "sp")


class Buf:
    __slots__ = ("name", "t", "w", "r", "dsem", "dcnt", "is_psum")

    def __init__(self, name, t=None):
        self.name = name
        self.t = t
        self.is_psum = False
        self.w = {}
        self.r = {}
        self.dsem = None
        self.dcnt = 0

    def __getitem__(self, k):
        return self.t[k]


class Prog:
    def __init__(self, nc, es, same_engine_sync=True):
        self.nc = nc
        self.es = es
        self.same = same_engine_sync
        self.esem = {e: es.enter_context(nc.semaphore("sem_" + e)) for e in ENGS}
        self.ecnt = {e: 0 for e in ENGS}
        self.q = {e: [] for e in ENGS}
        self.meta = {e: [] for e in ENGS}
        self.seen = {e: {} for e in ENGS}
        self.dcur = {}
        self.sem_pool = []
        self.scoped = []
        self.ninst = 0

    def sbuf(self, name, shape, dtype, es=None):
        self.uid = getattr(self, "uid", 0) + 1
        name = f"{name}_{self.uid}"
        t = (es or self.es).enter_context(self.nc.sbuf_tensor(name, list(shape), dtype))
        b = Buf(name, t)
        if es is not None:
            self.scoped.append(b)
        return b

    def psum(self, name, shape, dtype, es=None):
        self.uid = getattr(self, "uid", 0) + 1
        name = f"{name}_{self.uid}"
        t = (es or self.es).enter_context(self.nc.psum_tensor(name, list(shape), dtype))
        b = Buf(name, t)
        b.is_psum = True
        return b

    def dram(self, name, shape, dtype, kind="Internal"):
        t = self.nc.dram_tensor(name, list(shape), dtype, kind=kind)
        return Buf(name, t)

    def _dsem(self, b):
        if b.dsem is None:
            if self.sem_pool:
                b.dsem = self.sem_pool.pop()
                b.dcnt = self.dcur[b.dsem]
            else:
                b.dsem = self.es.enter_context(self.nc.semaphore("dsem_" + b.name))
                self.dcur[b.dsem] = 0
        return b.dsem

    def _waits(self, e, reads, writes, skip_own=None):
        deps = {}
        own_sem = self.esem[e]
        for b in reads:
            for s, v in b.w.items():
                if deps.get(s, 0) < v:
                    deps[s] = v
            if b.is_psum:
                for s, v in b.r.items():
                    if s is not own_sem and deps.get(s, 0) < v:
                        deps[s] = v
        for b in writes:
            for d in (b.w, b.r):
                for s, v in d.items():
                    if deps.get(s, 0) < v:
                        deps[s] = v
        seen = self.seen[e]
        own = self.esem[e]
        if skip_own is None:
            skip_own = (e == "pe") or not self.same
        waits = []
        for s, v in deps.items():
            if s is own and skip_own:
                continue
            if seen.get(s, 0) >= v:
                continue
            seen[s] = v
            waits.append((s, v))
        return waits

    def op(self, e, fn, reads=(), writes=(), inc=True):
        waits = self._waits(e, reads, writes)
        own = self.esem[e]
        cnt = self.ecnt[e] + 1
        if inc:
            self.ecnt[e] = cnt

        def emit(eng, waits=waits, fn=fn, inc=inc, own=own):
            for s, v in waits:
                eng.wait_ge(s, v)
            ins = fn(eng)
            if inc:
                ins.then_inc(own, 1)
        self.q[e].append(emit)
        self.meta[e].append((waits, (own, 1) if inc else None))
        self.ninst += 1
        for b in reads:
            if b.r.get(own, 0) < cnt:
                b.r[own] = cnt
        for b in writes:
            b.w = {own: cnt}
            b.r = {}

    def dma(self, qe, fn, owner, reads=(), writes=()):
        ds = self._dsem(owner)
        waits = self._waits(qe, reads, writes, skip_own=False)
        seen = self.seen[qe]
        if owner.dcnt > 0 and seen.get(ds, 0) < owner.dcnt:
            seen[ds] = owner.dcnt
            waits.append((ds, owner.dcnt))
        owner.dcnt += 16
        val = owner.dcnt
        self.dcur[ds] = val

        def emit(eng, waits=waits, fn=fn, ds=ds):
            for s, v in waits:
                eng.wait_ge(s, v)
            fn(eng).then_inc(ds, 16)
        self.q[qe].append(emit)
        self.meta[qe].append((waits, (ds, 16)))
        self.ninst += 1
        for b in reads:
            if b.r.get(ds, 0) < val:
                b.r[ds] = val
        for b in writes:
            b.w = {ds: val}
            b.r = {}

    def barrier(self, final=False):
        targets = [(self.esem[e], self.ecnt[e]) for e in ENGS if self.ecnt[e] > 0]
        skip = getattr(self, "bg_sems", ())
        targets += [(sm, v) for sm, v in self.dcur.items() if v > 0 and (final or sm not in skip)]
        for e in ENGS:
            seen = self.seen[e]
            waits = []
            for s, v in targets:
                if seen.get(s, 0) < v:
                    seen[s] = v
                    waits.append((s, v))

            def emit(eng, waits=waits):
                for s, v in waits:
                    eng.wait_ge(s, v)
            self.q[e].append(emit)
            self.meta[e].append((waits, None))
        for b in self.scoped:
            if b.dsem is not None:
                self.sem_pool.append(b.dsem)
                b.dsem = None
        self.scoped = []

    def check(self):
        val = {}
        pos = {e: 0 for e in ENGS}
        progress = True
        while progress:
            progress = False
            for e in ENGS:
                m = self.meta[e]
                while pos[e] < len(m):
                    waits, inc = m[pos[e]]
                    if any(val.get(id(s), 0) < v for s, v in waits):
                        break
                    if inc is not None:
                        val[id(inc[0])] = val.get(id(inc[0]), 0) + inc[1]
                    pos[e] += 1
                    progress = True
        stuck = {e: (pos[e], len(self.meta[e])) for e in ENGS if pos[e] < len(self.meta[e])}
        return stuck or None

    def emit(self):
        with self.nc.Block() as block:
            for e, deco in (("sp", block.sync), ("act", block.scalar), ("pe", block.tensor),
                            ("dve", block.vector), ("pool", block.gpsimd)):
                lst = self.q[e]
                if not lst:
                    continue

                def body(eng, lst=lst):
                    for f in lst:
                        f(eng)
                deco(body)


def mm(P, out_ap, lhsT, rhs, start, stop, reads, writes, inc):
    P.op("pe", lambda e: e.matmul(out_ap, lhsT, rhs, start=start, stop=stop),
         reads=reads, writes=writes, inc=inc)


def act(P, out_ap, in_ap, func, reads, writes, scale=None, bias=None, accum=None):
    kw = {}
    if scale is not None:
        kw["scale"] = scale
    if bias is not None:
        kw["bias"] = bias
    if accum is not None:
        kw["accum_out"] = accum
    P.op("act", lambda e: e.activation(out=out_ap, in_=in_ap, func=func, **kw), reads=reads, writes=writes)


class Ctx:
    pass


def rstd_from_ssq(P, C, ssq_ap, n, out_ap, ssq_buf, tmp_buf, out_buf):
    act(P, tmp_buf_ap(tmp_buf, ssq_ap), ssq_ap, AF.Sqrt, [ssq_buf, C.epsb], [tmp_buf],
        scale=1.0 / n, bias=C.epsb[0:ssq_ap.shape[0], 0:1])
    P.op("dve", lambda e: e.reciprocal(out=out_ap, in_=tmp_buf_ap(tmp_buf, ssq_ap)),
         reads=[tmp_buf], writes=[out_buf])


def tmp_buf_ap(tmp_buf, like):
    sh = like.shape
    if len(sh) == 2:
        return tmp_buf[0:sh[0], 0:sh[1]]
    return tmp_buf[0:sh[0], 0:sh[1], 0:sh[2]]


def norm_transpose_tile(P, C, es_bufs, src_ap_fn, gain, hT, nsub):
    B = es_bufs
    for s in range(nsub):
        xt = B.xt[s % 2]
        P.dma("sp", lambda e, xt=xt, s=s: e.dma_start(out=xt[:, :], in_=src_ap_fn(s)), owner=xt, writes=[xt])
        act(P, B.junk[:, :], xt[:, :], AF.Square, [xt], [B.junk, B.ssq], accum=B.ssq[:, 0:1])
        act(P, B.sd[:, 0:1], B.ssq[:, 0:1], AF.Ln, [B.ssq, C.epsb], [B.sd], scale=1.0 / D, bias=C.epsb[:, 0:1])
        act(P, B.rs[:, 0:1], B.sd[:, 0:1], AF.Exp, [B.sd], [B.rs], scale=-0.5)
        hn = B.hn[s % 2]
        P.op("dve", lambda e, xt=xt, hn=hn: e.scalar_tensor_tensor(
            out=hn[:, :], in0=xt[:, :], scalar=B.rs[:, 0:1], in1=gain[:, :], op0=ALU.mult, op1=ALU.mult),
            reads=[xt, B.rs, gain], writes=[hn])
        transpose_rows(P, C, B, hn, hT, s)


def transpose_rows(P, C, B, hn, hT, s):
    for j in range(2):
        tp = B.tp[j]
        for i in range(8):
            kc = j * 8 + i
            P.op("pe", lambda e, tp=tp, i=i, kc=kc, hn=hn: e.transpose(
                out=tp[:, i, :], in_=hn[:, kc * 128:(kc + 1) * 128], identity=C.ident[:, :]),
                reads=[hn, C.ident], writes=[tp], inc=(i == 7))
        if j == 0:
            act(P, hT[:, 0:8, s * 128:(s + 1) * 128], tp[:, :, :], AF.Copy, [tp], [hT])
        else:
            P.op("dve", lambda e, tp=tp: e.tensor_copy(out=hT[:, 8:16, s * 128:(s + 1) * 128], in_=tp[:, :, :]),
                 reads=[tp], writes=[hT])


def load_gain(P, C, es, pfx, idx):
    g = P.sbuf(f"{pfx}_gain", [128, D], F32, es)
    P.dma("sp", lambda e: e.dma_start(out=g[:, :], in_=C.gb_d.t.ap()[:, idx, :]), owner=g, writes=[g])
    return g


def alloc_norm_bufs(P, es, pfx):
    B = Ctx()
    B.xt = [P.sbuf(f"{pfx}_xt{i}", [128, D], F32, es) for i in range(2)]
    B.hn = [P.sbuf(f"{pfx}_hn{i}", [128, D], BF16, es) for i in range(2)]
    B.junk = P.sbuf(f"{pfx}_junk", [128, D], BF16, es)
    B.ssq = P.sbuf(f"{pfx}_ssq", [128, 1], F32, es)
    B.sd = P.sbuf(f"{pfx}_sd", [128, 1], F32, es)
    B.rs = P.sbuf(f"{pfx}_rs", [128, 1], F32, es)
    B.tp = [P.psum(f"{pfx}_tp{i}", [128, 8, 128], BF16, es) for i in range(2)]
    return B


def head_norm_fm(P, C, A, pm, gcol, dst_ap, dst_buf):
    act(P, A.sqb[:, :], pm[:, :], AF.Square, [pm], [A.sqb])
    ps = A.ps[A.psi % 2]
    A.psi += 1
    mm(P, ps[:, :], C.ones[:, :], A.sqb[:, :], True, True, [C.ones, A.sqb], [ps], True)
    act(P, A.sd[:, :], ps[:, :], AF.Sqrt, [ps, C.epsb], [A.sd], scale=1.0 / 128, bias=C.epsb[:, 0:1])
    P.op("dve", lambda e: e.reciprocal(out=A.rinv[:, :], in_=A.sd[:, :]), reads=[A.sd], writes=[A.rinv])
    P.op("dve", lambda e: e.scalar_tensor_tensor(out=dst_ap, in0=pm[:, :], scalar=gcol, in1=A.rinv[:, :],
                                                  op0=ALU.mult, op1=ALU.mult),
         reads=[pm, A.rinv, C.hg], writes=[dst_buf])


def phase_A(P, C, L, x_src, tiles):
    with ExitStack() as es:
        B = alloc_norm_bufs(P, es, "A")
        A = Ctx()
        A.hT = [P.sbuf(f"A_hT{i}", [128, 16, T], BF16, es) for i in range(2)]
        A.wt = [P.sbuf(f"A_wt{i}", [128, 16, 256], BF16, es) for i in range(3)]
        A.sqb = [P.sbuf(f"A_sqb{i}", [128, T], BF16, es) for i in range(2)]
        A.sd = [P.sbuf(f"A_sdw{i}", [128, T], F32, es) for i in range(2)]
        A.rinv = [P.sbuf(f"A_rinv{i}", [128, T], F32, es) for i in range(2)]
        A.ob = [P.sbuf(f"A_ob{i}", [128, T], BF16, es) for i in range(3)]
        A.vb = [P.sbuf(f"A_vb{i}", [128, 4, 256], BF16, es) for i in range(2)]
        A.pm = [P.psum(f"A_pm{i}", [128, T], F32, es) for i in range(3)]
        A.pv = P.psum("A_pv", [128, 2, 256], F32, es)
        A.ps = [P.psum(f"A_ps{i}", [128, T], F32, es) for i in range(2)]
        w_bf = C.wbf[("w_in", L)]
        w_view = w_bf.t.ap().rearrange("(kc p) n -> p kc n", p=128)
        gain = load_gain(P, C, es, "A", L * 4 + 0)

        xt4 = B.xt + [P.sbuf(f"A_xtx{i}", [128, D], F32, es) for i in range(2)]
        hn4 = B.hn + [P.sbuf(f"A_hnx{i}", [128, D], BF16, es) for i in range(2)]

        def pro_norm(it):
            ti, _ = tiles[it]
            t0 = ti * T
            for s in range(4):
                xt, hn = xt4[s], hn4[s]
                P.dma("sp", lambda e, xt=xt, s=s: e.dma_start(out=xt[:, :], in_=x_src.t.ap()[t0 + s * 128:t0 + (s + 1) * 128, :]),
                      owner=xt, reads=[x_src], writes=[xt])
                act(P, B.junk[:, :], xt[:, :], AF.Square, [xt], [B.junk, B.ssq], accum=B.ssq[:, 0:1])
                act(P, B.sd[:, 0:1], B.ssq[:, 0:1], AF.Ln, [B.ssq, C.epsb], [B.sd], scale=1.0 / D, bias=C.epsb[:, 0:1])
                act(P, B.rs[:, 0:1], B.sd[:, 0:1], AF.Exp, [B.sd], [B.rs], scale=-0.5)
                P.op("dve", lambda e, xt=xt, hn=hn: e.scalar_tensor_tensor(
                    out=hn[:, :], in0=xt[:, :], scalar=B.rs[:, 0:1], in1=gain[:, :], op0=ALU.mult, op1=ALU.mult),
                    reads=[xt, B.rs, gain], writes=[hn])

        def pro_tr(it):
            for s in range(4):
                transpose_rows(P, C, B, hn4[s], A.hT[it % 2], s)

        units = []
        cnt = {"w": 0, "pm": 0, "ob": 0, "vb": 0, "u": 0}
        for it, (ti, kinds) in enumerate(tiles):
            t0 = ti * T
            hT = A.hT[it % 2]
            first = len(units)
            for blk in range(16):
                kind = "qqqqkkkkvvvvuumm"[blk]
                if kind not in kinds:
                    continue
                wt = A.wt[cnt["w"] % 3]
                cnt["w"] += 1

                def load_w(wt=wt, blk=blk):
                    P.dma("sp", lambda e: e.dma_start(out=wt[:, :, :], in_=w_view[:, :, blk * 256:(blk + 1) * 256]),
                          owner=wt, reads=[w_bf], writes=[wt])
                if kind == "v":
                    vb = A.vb[cnt["vb"] % 2]
                    cnt["vb"] += 1

                    def s1(load_w=load_w, vb=vb, hT=hT, wt=wt, blk=blk, t0=t0):
                        load_w()
                        pv = A.pv
                        for s in range(4):
                            for kc in range(16):
                                mm(P, pv[:, s % 2, :], hT[:, kc, s * 128:(s + 1) * 128], wt[:, kc, :], kc == 0, kc == 15,
                                   [hT, wt], [pv], kc == 15)
                            if s % 2 == 0:
                                act(P, vb[:, s, :], pv[:, s % 2, :], AF.Copy, [pv], [vb])
                            else:
                                P.op("dve", lambda e, s=s: e.tensor_copy(out=vb[:, s, :], in_=pv[:, s % 2, :]),
                                     reads=[pv], writes=[vb])
                        c0 = (blk - 8) * 256
                        P.dma("act", lambda e: e.dma_start(
                            out=C.v_s[L].t.ap()[t0:t0 + T, c0:c0 + 256].rearrange("(s p) n -> p s n", p=128), in_=vb[:, :, :]),
                            owner=vb, reads=[vb], writes=[C.v_s[L]])
                    units.append([s1, None, None])
                    continue
                for half in range(2):
                    cc = blk * 2 + half
                    pm = A.pm[cnt["pm"] % 3]
                    cnt["pm"] += 1
                    ob = A.ob[cnt["ob"] % 3]
                    cnt["ob"] += 1
                    ui = cnt["u"]
                    cnt["u"] += 1
                    if kind == "u":
                        dst, r0, gcol = C.u_s[L], (cc - 24) * 128, None
                    elif kind == "q":
                        dst, r0, gcol = C.q_s[L], cc * 128, L * 4 + 0
                    elif kind == "k":
                        dst, r0, gcol = C.k_s[L], (cc - 8) * 128, L * 4 + 1
                    else:
                        dst, r0, gcol = C.m_s[L], (cc - 28) * 128, L * 4 + 2

                    def s1(load_w=load_w if half == 0 else None, pm=pm, hT=hT, wt=wt, half=half):
                        if load_w is not None:
                            load_w()
                        for kc in range(16):
                            mm(P, pm[:, :], wt[:, kc, half * 128:(half + 1) * 128], hT[:, kc, :], kc == 0, kc == 15,
                               [hT, wt], [pm], kc == 15)

                    def s2(pm=pm, ob=ob, gcol=gcol, ui=ui):
                        if gcol is None:
                            act(P, ob[:, :], pm[:, :], AF.Copy, [pm], [ob])
                            return
                        sqb, ps = A.sqb[ui % 2], A.ps[ui % 2]
                        act(P, sqb[:, :], pm[:, :], AF.Square, [pm], [sqb])
                        mm(P, ps[:, :], C.ones[:, :], sqb[:, :], True, True, [C.ones, sqb], [ps], True)

                    def s3(pm=pm, ob=ob, gcol=gcol, ui=ui, dst=dst, r0=r0, t0=t0):
                        if gcol is not None:
                            ps, sd, rinv = A.ps[ui % 2], A.sd[ui % 2], A.rinv[ui % 2]
                            act(P, sd[:, :], ps[:, :], AF.Ln, [ps, C.epsb], [sd], scale=1.0 / 128, bias=C.epsb[:, 0:1])
                            act(P, rinv[:, :], sd[:, :], AF.Exp, [sd], [rinv], scale=-0.5)
                            P.op("dve", lambda e: e.scalar_tensor_tensor(
                                out=ob[:, :], in0=pm[:, :], scalar=C.hg[:, gcol:gcol + 1], in1=rinv[:, :],
                                op0=ALU.mult, op1=ALU.mult), reads=[pm, rinv, C.hg], writes=[ob])
                        P.dma("act", lambda e: e.dma_start(out=dst.t.ap()[r0:r0 + 128, t0:t0 + T], in_=ob[:, :]),
                              owner=ob, reads=[ob], writes=[dst])
                    units.append([s1, s2, s3])
            if it + 1 < len(tiles):
                n = len(units) - first
                for frac, fn in ((n // 4, pro_norm), ((3 * n) // 4, pro_tr)):
                    pos = min(first + frac, len(units) - 1)
                    old_fn = units[pos][0]

                    def s1_with_pro(old_fn=old_fn, fn=fn, nit=it + 1):
                        fn(nit)
                        old_fn()
                    units[pos][0] = s1_with_pro
        pro_norm(0)
        pro_tr(0)
        run_pipeline(units, 3)
    P.barrier()


def run_pipeline(units, nstages):
    n = len(units)
    for step in range(n + nstages - 1):
        for st in range(nstages):
            u = step - st
            if 0 <= u < n and units[u][st] is not None:
                units[u][st]()


def fin_a(P, C, Fz, po, po_buf, has_den, n=128):
    if has_den:
        if has_den == "copy":
            P.op("dve", lambda e: e.tensor_copy(out=Fz.yb[0:n, :], in_=po[0:n, 0:128]), reads=[po_buf], writes=[Fz.yb])
        else:
            P.op("dve", lambda e: e.reciprocal(out=Fz.rden[0:n, 0:1], in_=po[0:n, 128:129]), reads=[po_buf], writes=[Fz.rden])
            P.op("dve", lambda e: e.tensor_scalar(out=Fz.yb[0:n, :], in0=po[0:n, 0:128], scalar1=Fz.rden[0:n, 0:1],
                                                   scalar2=None, op0=ALU.mult), reads=[po_buf, Fz.rden], writes=[Fz.yb])
        src, src_buf = Fz.yb[0:n, :], Fz.yb
        P.op("dve", lambda e: e.scalar_tensor_tensor(out=Fz.junk[0:n, :], in0=src, scalar=1.0, in1=src, op0=ALU.mult,
                                                      op1=ALU.mult, accum_out=Fz.ssq[0:n, 0:1]),
             reads=[src_buf], writes=[Fz.junk, Fz.ssq])
    else:
        src, src_buf = po[0:n, 0:128], po_buf
        act(P, Fz.junk[0:n, :], src, AF.Square, [src_buf], [Fz.junk, Fz.ssq], accum=Fz.ssq[0:n, 0:1])
    act(P, Fz.sd[0:n, 0:1], Fz.ssq[0:n, 0:1], AF.Ln, [Fz.ssq, C.epsb], [Fz.sd], scale=1.0 / 128, bias=C.epsb[0:n, 0:1])
    act(P, Fz.rs[0:n, 0:1], Fz.sd[0:n, 0:1], AF.Exp, [Fz.sd], [Fz.rs], scale=-0.5)


def fin_b(P, C, Fz, po, po_buf, has_den, gain_ap, gain_buf, dst_ap, dst_buf, n=128):
    if has_den:
        src, src_buf = Fz.yb[0:n, :], Fz.yb
    else:
        src, src_buf = po[0:n, 0:128], po_buf
    P.op("dve", lambda e: e.scalar_tensor_tensor(out=dst_ap, in0=src, scalar=Fz.rs[0:n, 0:1], in1=gain_ap,
                                                  op0=ALU.mult, op1=ALU.mult),
         reads=[src_buf, Fz.rs, gain_buf], writes=[dst_buf])


def finalize_head(P, C, Fz, po, po_buf, has_den, gain_ap, gain_buf, dst_ap, dst_buf, npart=128):
    n = npart
    if has_den:
        P.op("dve", lambda e: e.reciprocal(out=Fz.rden[0:n, 0:1], in_=po[0:n, 128:129]), reads=[po_buf], writes=[Fz.rden])
        P.op("dve", lambda e: e.tensor_scalar(out=Fz.yb[0:n, :], in0=po[0:n, 0:128], scalar1=Fz.rden[0:n, 0:1], scalar2=None,
                                               op0=ALU.mult), reads=[po_buf, Fz.rden], writes=[Fz.yb])
        src, src_buf = Fz.yb[0:n, :], Fz.yb
    else:
        src, src_buf = po[0:n, 0:128], po_buf
    act(P, Fz.junk[0:n, :], src, AF.Square, [src_buf], [Fz.junk, Fz.ssq], accum=Fz.ssq[0:n, 0:1])
    act(P, Fz.sd[0:n, 0:1], Fz.ssq[0:n, 0:1], AF.Ln, [Fz.ssq, C.epsb], [Fz.sd], scale=1.0 / 128, bias=C.epsb[0:n, 0:1])
    act(P, Fz.rs[0:n, 0:1], Fz.sd[0:n, 0:1], AF.Exp, [Fz.sd], [Fz.rs], scale=-0.5)
    P.op("dve", lambda e: e.scalar_tensor_tensor(out=dst_ap, in0=src, scalar=Fz.rs[0:n, 0:1], in1=gain_ap,
                                                  op0=ALU.mult, op1=ALU.mult),
         reads=[src_buf, Fz.rs, gain_buf], writes=[dst_buf])


def alloc_finalize(P, es, pfx):
    Fz = Ctx()
    Fz.rden = P.sbuf(f"{pfx}_rden", [128, 1], F32, es)
    Fz.yb = P.sbuf(f"{pfx}_yb", [128, 128], F32, es)
    Fz.junk = P.sbuf(f"{pfx}_fjunk", [128, 128], BF16, es)
    Fz.ssq = P.sbuf(f"{pfx}_fssq", [128, 1], F32, es)
    Fz.sd = P.sbuf(f"{pfx}_fsd", [128, 1], F32, es)
    Fz.rs = P.sbuf(f"{pfx}_frs", [128, 1], F32, es)
    return Fz


def phase_X(P, C, L, blocks):
    with ExitStack() as es:
        gain_o = load_gain(P, C, es, "Xo", L * 4 + 2)
        kmT = P.sbuf("X_kmT", [128, 4, 256], BF16, es)
        vme = P.sbuf("X_vme", [128, 2, 4, 129], BF16, es)
        qm = [P.sbuf(f"X_qm{i}", [128, 4, T], BF16, es) for i in range(2)]
        pT = [P.sbuf(f"X_pT{i}", [128, 2, T], BF16, es) for i in range(3)]
        ynt = [P.sbuf(f"X_ynt{i}", [128, 4, 512], BF16, es) for i in range(2)]
        FZ = alloc_finalize_sets(P, es, "X", 8)
        P.op("dve", lambda e: e.memset(vme[:, :, :, :], 1.0), writes=[vme])
        with ExitStack() as es2:
            B = alloc_norm_bufs(P, es2, "X")
            gain_m = load_gain(P, C, es2, "Xm", L * 4 + 1)
            hTm = P.sbuf("X_hTm", [128, 16, 256], BF16, es2)
            wts = [P.sbuf(f"X_wt{i}", [128, 16, 256], BF16, es2) for i in range(2)]
            sqb = P.sbuf("X_sqb", [128, 256], BF16, es2)
            sdw = P.sbuf("X_sdw", [128, 256], F32, es2)
            rinv = P.sbuf("X_rinv", [128, 256], F32, es2)
            ps = P.psum("X_ps", [128, T], F32, es2)
            pmk = [P.psum(f"X_pmk{i}", [128, T], F32, es2) for i in range(3)]
            norm_transpose_tile(P, C, B, lambda s: C.mem.t.ap()[s * 128:(s + 1) * 128, :], gain_m, hTm, 2)
            w_bf = C.wbf[("w_mem_kv", L)]
            w_view = w_bf.t.ap().rearrange("(kc p) n -> p kc n", p=128)
            pmi = 0
            for blk in range(4):
                wt = wts[blk % 2]
                P.dma("sp", lambda e, wt=wt, blk=blk: e.dma_start(out=wt[:, :, :], in_=w_view[:, :, blk * 256:(blk + 1) * 256]),
                      owner=wt, reads=[w_bf], writes=[wt])
                if blk < 2:
                    for half in range(2):
                        h = blk * 2 + half
                        pm = pmk[pmi % 3]
                        pmi += 1
                        for kc in range(16):
                            mm(P, pm[:, 0:256], wt[:, kc, half * 128:(half + 1) * 128], hTm[:, kc, :], kc == 0, kc == 15,
                               [hTm, wt], [pm], kc == 15)
                        act(P, sqb[:, :], pm[:, 0:256], AF.Square, [pm], [sqb])
                        mm(P, ps[:, 0:256], C.ones[:, :], sqb[:, :], True, True, [C.ones, sqb], [ps], True)
                        act(P, sdw[:, :], ps[:, 0:256], AF.Ln, [ps, C.epsb], [sdw], scale=1.0 / 128, bias=C.epsb[:, 0:1])
                        act(P, rinv[:, :], sdw[:, :], AF.Exp, [sdw], [rinv], scale=-0.5)
                        P.op("dve", lambda e, pm=pm, h=h: e.scalar_tensor_tensor(
                            out=kmT[:, h, :], in0=pm[:, 0:256], scalar=C.hg[:, L * 4 + 3:L * 4 + 4], in1=rinv[:, :],
                            op0=ALU.mult, op1=ALU.mult), reads=[pm, rinv, C.hg], writes=[kmT])
                else:
                    for s_ in range(2):
                        pm = pmk[pmi % 3]
                        pmi += 1
                        for kc in range(16):
                            mm(P, pm[:, 0:256], hTm[:, kc, s_ * 128:(s_ + 1) * 128], wt[:, kc, :], kc == 0, kc == 15,
                               [hTm, wt], [pm], kc == 15)
                        h0 = (blk - 2) * 2
                        P.op("dve", lambda e, pm=pm, s_=s_, h0=h0: e.tensor_copy(
                            out=vme[:, s_, h0:h0 + 2, 0:128], in_=pm[:, 0:256].rearrange("p (h d) -> p h d", h=2)),
                            reads=[pm], writes=[vme])
        P.barrier()
        pms = [P.psum(f"X_pm{i}", [128, T], F32, es) for i in range(4)]
        pos = [P.psum(f"X_po{i}", [128, 129], F32, es) for i in range(4)]
        m_view = C.m_s[L].t.ap().rearrange("(h p) t -> p h t", p=128)

        def load_q(ib):
            q = qm[ib % 2]
            t0 = blocks[ib] * T
            P.dma("sp", lambda e: e.dma_start(out=q[:, :, :], in_=m_view[:, :, t0:t0 + T]),
                  owner=q, reads=[C.m_s[L]], writes=[q])

        units = []
        for ib, tb in enumerate(blocks):
            t0 = tb * T
            q = qm[ib % 2]
            yt = ynt[ib % 2]
            for h in range(4):
                u = len(units)
                pp = [pms[(u % 2) * 2], pms[(u % 2) * 2 + 1]]
                pt = pT[u % 3]
                fzs = [FZ[(u % 2) * 4 + sub] for sub in range(4)]
                pre = (ib + 1) if (h == 0 and ib + 1 < len(blocks)) else None

                def s1(pre=pre, pp=pp, q=q, h=h):
                    if pre is not None:
                        load_q(pre)
                    for kt in range(2):
                        mm(P, pp[kt][:, :], kmT[:, h, kt * 128:(kt + 1) * 128], q[:, h, :], True, True, [kmT, q], [pp[kt]], True)

                def s2(pp=pp, pt=pt):
                    for kt in range(2):
                        act(P, pt[:, kt, :], pp[kt][:, :], AF.Exp, [pp[kt]], [pt])

                def s3(pt=pt, fzs=fzs, h=h):
                    for sub in range(4):
                        po = pos[sub]
                        for kt in range(2):
                            mm(P, po[:, :], pt[:, kt, sub * 128:(sub + 1) * 128], vme[:, kt, h, :], kt == 0, kt == 1,
                               [pt, vme], [po], kt == 1)
                        fin_a(P, C, fzs[sub], po, po, True)

                def s4(fzs=fzs, h=h, yt=yt, t0=t0):
                    c0 = 1536 + h * 128
                    for sub in range(4):
                        fin_b(P, C, fzs[sub], pos[sub], pos[sub], True, gain_o[:, c0:c0 + 128], gain_o,
                              yt[:, sub, h * 128:(h + 1) * 128], yt)
                    if h == 3:
                        P.dma("act", lambda e: e.dma_start(
                            out=C.yn_s[L].t.ap()[t0:t0 + T, 1536:2048].rearrange("(s p) n -> p s n", p=128), in_=yt[:, :, :]),
                            owner=yt, reads=[yt], writes=[C.yn_s[L]])
                units.append([s1, s2, s3, s4])
        load_q(0)
        run_pipeline(units, 4)
    P.barrier()


SPECIAL = {0: 1, 1: 2, 30: 3, 31: 4, 32: 5, 33: 6, 62: 7, 63: 8}


def na_slots(bp):
    if bp in (0, 32):
        return list(range(-2, 4))
    if bp in (31, 63):
        return list(range(-3, 3))
    return list(range(-2, 3))


def alloc_finalize_sets(P, es, pfx, n):
    return [alloc_finalize(P, es, f"{pfx}{i}") for i in range(n)]


def phase_N(P, C, L, blocks):
    with ExitStack() as es:
        FZ = alloc_finalize_sets(P, es, "N", 3)
        gain_o = load_gain(P, C, es, "No", L * 4 + 2)
        tz = P.sbuf("N_tz", [128, 7, 8, 128], BF16, es)
        tzs = [P.sbuf(f"N_tzs{i}", [128, 8, 128], F32, es) for i in range(2)]
        mk = P.sbuf("N_mk", [128, 63, 128], BF16, es)
        P.dma("sp", lambda e: e.dma_start(out=mk[:, :, :], in_=C.mk_d.t.ap()), owner=mk, writes=[mk])
        for d in range(7):
            st = tzs[d % 2]
            P.dma("sp", lambda e, st=st, d=d: e.dma_start(out=st[:, :, :], in_=C.tz_d.t.ap()[L, :, d, :, :]),
                  owner=st, writes=[st])
            act(P, tz[:, d, :, :], st[:, :, :], AF.Copy, [st], [tz])
        tzg = P.sbuf("N_tzg", [128, 5, 8, 128], BF16, es)
        for d in range(5):
            for h in range(8):
                P.op("dve", lambda e, d=d, h=h: e.tensor_tensor(out=tzg[:, d, h, :], in0=tz[:, d + 1, h, :],
                                                                 in1=mk[:, d + 1, :], op=ALU.add),
                     reads=[tz, mk], writes=[tzg])
        Q = [P.sbuf(f"N_q{i}", [128, 8, T], BF16, es) for i in range(2)]
        Kt = [P.sbuf(f"N_k{i}", [128, 8, 1024], BF16, es) for i in range(2)]
        V = [[P.sbuf(f"N_v{i}_{j}", [128, 8, 129], BF16, es) for j in range(8)] for i in range(2)]
        pT = [P.sbuf(f"N_pT{i}", [128, 8, 128], BF16, es) for i in range(3)]
        ynt = [P.sbuf(f"N_ynt{i}", [128, 4, 1024], BF16, es) for i in range(2)]
        psc = [P.psum(f"N_psc{i}", [128, 8, 128], F32, es) for i in range(3)]
        pos = [P.psum(f"N_po{i}", [128, 129], F32, es) for i in range(2)]
        for vv in V:
            for v in vv:
                P.op("dve", lambda e, v=v: e.memset(v[:, :, :], 1.0), writes=[v])
        q_view = C.q_s[L].t.ap().rearrange("(h p) t -> p h t", p=128)
        k_view = C.k_s[L].t.ap().rearrange("(h p) t -> p h t", p=128)
        v_view = C.v_s[L].t.ap().rearrange("(j p) (h d) -> p j h d", p=128, d=128)

        def loads(ib):
            tb = blocks[ib]
            t0 = tb * T
            q, kt, vv = Q[ib % 2], Kt[ib % 2], V[ib % 2]
            P.dma("sp", lambda e: e.dma_start(out=q[:, :, :], in_=q_view[:, :, t0:t0 + T]),
                  owner=q, reads=[C.q_s[L]], writes=[q])
            p_lo = tb * 4 - 2
            j = 0
            while j < 8:
                p = (p_lo + j) % 64
                n = min(8 - j, 64 - p)
                P.dma("sp", lambda e, j=j, p=p, n=n: e.dma_start(
                    out=kt[:, :, j * 128:(j + n) * 128], in_=k_view[:, :, p * 128:(p + n) * 128]),
                    owner=kt, reads=[C.k_s[L]], writes=[kt])
                for jj in range(n):
                    v = vv[j + jj]
                    P.dma("sp", lambda e, v=v, p=p, jj=jj: e.dma_start(out=v[:, :, 0:128], in_=v_view[:, p + jj, :, :]),
                          owner=v, reads=[C.v_s[L]], writes=[v])
                j += n

        units = []
        for ib, tb in enumerate(blocks):
            t0 = tb * T
            q, kt, vv = Q[ib % 2], Kt[ib % 2], V[ib % 2]
            yt = ynt[ib % 2]
            for sb in range(4):
                bp = tb * 4 + sb
                slots = na_slots(bp)
                cls = SPECIAL.get(bp, 0)
                ns = len(slots)
                for h in range(8):
                    u = len(units)
                    pa = psc[u % 3]
                    pt = pT[u % 3]
                    po = pos[u % 2]
                    Fz = FZ[u % 3]
                    pre = (ib + 1) if (sb == 0 and h == 5 and ib + 1 < len(blocks)) else None
                    last = (sb == 3 and h == 7)

                    def s1(pre=pre, pa=pa, kt=kt, q=q, slots=slots, sb=sb, h=h, cls=cls):
                        if pre is not None:
                            loads(pre)
                        for si, dl in enumerate(slots):
                            j = sb + dl + 2
                            pp = pa
                            o_ap = pp[:, si, :]
                            mm(P, o_ap, kt[:, h, j * 128:(j + 1) * 128], q[:, h, sb * 128:(sb + 1) * 128], True, False,
                               [kt, q], [pp], False)
                            fin = (si == len(slots) - 1)
                            if cls == 0:
                                mm(P, o_ap, C.ident[:, :], tzg[:, dl + 2, h, :], False, True, [C.ident, tzg], [pp], fin)
                            else:
                                mm(P, o_ap, C.ident[:, :], tz[:, dl + 3, h, :], False, False, [C.ident, tz], [pp], False)
                                mm(P, o_ap, C.ident[:, :], mk[:, cls * 7 + dl + 3, :], False, True, [C.ident, mk], [pp], fin)

                    def s2(pa=pa, pt=pt, po=po, vv=vv, slots=slots, ns=ns, sb=sb, h=h):
                        act(P, pt[:, 0:ns, :], pa[:, 0:ns, :], AF.Exp, [pa], [pt])
                        for si, dl in enumerate(slots):
                            v = vv[sb + dl + 2]
                            mm(P, po[:, :], pt[:, si, :], v[:, h, :], si == 0, si == ns - 1, [pt, v], [po], si == ns - 1)

                    def s3(Fz=Fz, po=po):
                        fin_a(P, C, Fz, po, po, True)

                    def s4(Fz=Fz, po=po, yt=yt, sb=sb, h=h, last=last, t0=t0):
                        fin_b(P, C, Fz, po, po, True, gain_o[:, h * 128:(h + 1) * 128], gain_o,
                              yt[:, sb, h * 128:(h + 1) * 128], yt)
                        if last:
                            P.dma("act", lambda e: e.dma_start(
                                out=C.yn_s[L].t.ap()[t0:t0 + T, 0:1024].rearrange("(s p) n -> p s n", p=128), in_=yt[:, :, :]),
                                owner=yt, reads=[yt], writes=[C.yn_s[L]])
                    units.append([s1, s2, s3, s4])
        loads(0)
        run_pipeline(units, 4)
    P.barrier()


def phase_F(P, C, L, nk2):
    with ExitStack() as es:
        FZ = alloc_finalize_sets(P, es, "F", 3)
        gain_o = load_gain(P, C, es, "Fo", L * 4 + 2)
        dft = P.sbuf("F_dft", [64, 128], BF16, es)
        ccsc = P.sbuf("F_ccsc", [128, 2, 128], F32, es)
        wf = P.sbuf("F_wf", [128, 4, 128], F32, es)
        ab = P.sbuf("F_ab", [128, 4, 2, 128], BF16, es)
        uA = [P.sbuf(f"F_uA{i}", [64, 32, 128], BF16, es) for i in range(2)]
        Z = P.sbuf("F_Z", [128, 64, 2, 128], BF16, es)
        mbt = [P.sbuf(f"F_mbt{i}", [128, 8, 2, 2, 128], BF16, es) for i in range(2)]
        xT = [P.sbuf(f"F_xT{i}", [128, 2, 128], BF16, es) for i in range(3)]
        ynf = [P.sbuf(f"F_ynf{i}", [128, 8, 128], BF16, es) for i in range(2)]
        pz = [P.psum(f"F_pz{i}", [128, 4, 128], F32, es) for i in range(2)]
        px = [P.psum(f"F_px{i}", [128, 2, 128], F32, es) for i in range(2)]
        py = [P.psum(f"F_py{i}", [128, 128], F32, es) for i in range(4)]
        P.dma("sp", lambda e: e.dma_start(out=dft[:, :], in_=C.dft_d.t.ap()), owner=dft, writes=[dft])
        P.dma("sp", lambda e: e.dma_start(out=ccsc[:, :, :], in_=C.ccsc_d.t.ap()), owner=ccsc, writes=[ccsc])
        P.dma("sp", lambda e: e.dma_start(out=wf[:, :, :], in_=C.wf_d.t.ap()[L].rearrange("g c e -> c g e")),
              owner=wf, writes=[wf])
        for g in range(4):
            for w in range(2):
                pp = py[(g * 2 + w) % 4]
                mm(P, pp[:, :], ccsc[:, w, :], wf[:, g, :], True, True, [ccsc, wf], [pp], True)
                act(P, ab[:, g, w, :], pp[:, :], AF.Copy, [pp], [ab], scale=1.0 / 1024.0)
        u_view = C.u_s[L].t.ap().rearrange("c (n1 n2) -> n1 c n2", n2=128)
        y_view = C.yn_s[L].t.ap().rearrange("(k2 k1) e -> k2 k1 e", k1=64)
        mb_view = C.mb_d.t.ap()
        st = {"iz": 0}

        def stage_A(g):
            for qd in range(4):
                ua = uA[(g * 4 + qd) % 2]
                c0 = g * 128 + qd * 32
                P.dma("sp", lambda e, ua=ua, c0=c0: e.dma_start(out=ua[:, :, :], in_=u_view[:, c0:c0 + 32, :]),
                      owner=ua, reads=[C.u_s[L]], writes=[ua])
                for c4 in range(8):
                    pp = pz[st["iz"] % 2]
                    st["iz"] += 1
                    for i in range(4):
                        c = c4 * 4 + i
                        mm(P, pp[:, i, :], ua[:, c, :], dft[:, :], True, True, [ua, dft], [pp], i == 3)
                    cz = qd * 32 + c4 * 4
                    dst = Z[:, :, :, cz:cz + 4].rearrange("p k r c -> p c r k")
                    src = pp[:, :, :].rearrange("p c (r k) -> p c r k", r=2)
                    if c4 % 2 == 0:
                        act(P, dst, src, AF.Copy, [pp], [Z])
                    else:
                        P.op("dve", lambda e, dst=dst, src=src: e.tensor_copy(out=dst, in_=src), reads=[pp], writes=[Z])

        def load_mb(ch):
            mt = mbt[ch % 2]
            kc8 = ch % 8
            P.dma("sp", lambda e: e.dma_start(out=mt[:, :, :, :, :], in_=mb_view[:, kc8 * 8:(kc8 + 1) * 8, :, :, :]),
                  owner=mt, reads=[C.mb_d], writes=[mt])

        units = []
        for g in range(4):
            for kc8 in range(8):
                ch = g * 8 + kc8
                mt = mbt[ch % 2]
                yf = ynf[ch % 2]
                for i in range(8):
                    u = len(units)
                    k1 = kc8 * 8 + i
                    pxx = px[u % 2]
                    xt = xT[u % 3]
                    pyy = py[u % 4]
                    Fz = FZ[u % 3]
                    c0 = 1024 + g * 128

                    def s1(g=g, kc8=kc8, i=i, ch=ch, mt=mt, pxx=pxx, k1=k1):
                        if kc8 == 0 and i == 0:
                            stage_A(g)
                        if i == 0 and ch + 1 < 32:
                            load_mb(ch + 1)
                        mm(P, pxx[:, :, 0:nk2], Z[:, k1, 0, :], mt[:, i, 0, :, 0:nk2], True, False, [Z, mt], [pxx], False)
                        mm(P, pxx[:, :, 0:nk2], Z[:, k1, 1, :], mt[:, i, 1, :, 0:nk2], False, True, [Z, mt], [pxx], True)

                    def s2(g=g, i=i, pxx=pxx, xt=xt, pyy=pyy):
                        act(P, xt[:, :, 0:nk2], pxx[:, :, 0:nk2], AF.Copy, [pxx], [xt])
                        mm(P, pyy[0:nk2, :], xt[:, 0, 0:nk2], ab[:, g, 0, :], True, False, [xt, ab], [pyy], False)
                        mm(P, pyy[0:nk2, :], xt[:, 1, 0:nk2], ab[:, g, 1, :], False, True, [xt, ab], [pyy], True)

                    def s3(Fz=Fz, pyy=pyy, i=i):
                        fin_a(P, C, Fz, pyy, pyy, "copy", n=nk2)

                    def s4(Fz=Fz, pyy=pyy, i=i, yf=yf, c0=c0, kc8=kc8):
                        fin_b(P, C, Fz, pyy, pyy, "copy", gain_o[0:nk2, c0:c0 + 128], gain_o,
                              yf[0:nk2, i, :], yf, n=nk2)
                        if i == 7:
                            P.dma("act", lambda e: e.dma_start(
                                out=y_view[0:nk2, kc8 * 8:(kc8 + 1) * 8, c0:c0 + 128], in_=yf[0:nk2, :, :]),
                                owner=yf, reads=[yf], writes=[C.yn_s[L]])
                    units.append([s1, s2, s3, s4])
        load_mb(0)
        run_pipeline(units, 4)
    P.barrier()


def phase_O(P, C, L, tiles, x_src, x_dst):
    with ExitStack() as es:
        gain_f = load_gain(P, C, es, "Of", L * 4 + 3)
        hn = [P.sbuf(f"O_hn{i}", [128, D], BF16, es) for i in range(4)]
        B = Ctx()
        B.tp = [P.psum(f"O_tp{i}", [128, 8, 128], BF16, es) for i in range(2)]
        ssqs = [P.sbuf(f"O_ssq{i}", [128, 1], F32, es) for i in range(4)]
        sds = [P.sbuf(f"O_sd{i}", [128, 1], F32, es) for i in range(4)]
        rss = [P.sbuf(f"O_rs{i}", [128, 1], F32, es) for i in range(4)]
        hT = P.sbuf("O_hT", [128, 16, T], BF16, es)
        xms = [P.sbuf(f"O_xm{i}", [128, D], F32, es) for i in range(4)]
        aT = P.sbuf("O_aT", [128, NFF, T], BF16, es)
        wbig = [P.sbuf(f"O_wbig{i}", [128, 22, 512], BF16, es) for i in range(2)]
        wgu = [P.sbuf(f"O_wgu{i}", [128, 16, 256], BF16, es) for i in range(4)]
        sg = [P.sbuf(f"O_sg{i}", [128, T], F32, es) for i in range(2)]
        pa = [P.psum(f"O_pa{i}", [128, T], F32, es) for i in range(6)]
        wo_view = C.wbf[("w_out", L)].t.ap().rearrange("(kc p) n -> p kc n", p=128)
        wg_view = C.wbf[("w_gate", L)].t.ap().rearrange("(kc p) n -> p kc n", p=128)
        wu_view = C.wbf[("w_up", L)].t.ap().rearrange("(kc p) n -> p kc n", p=128)
        wd_view = C.wbf[("w_down", L)].t.ap().rearrange("(fc p) n -> p fc n", p=128)
        ipa = 0
        ibig = 0
        igu = 0
        for ti in tiles:
            t0 = ti * T
            for s in range(4):
                h = hn[s]
                P.dma("sp", lambda e, h=h, s=s, t0=t0: e.dma_start(
                    out=h[:, :], in_=C.yn_s[L].t.ap()[t0 + s * 128:t0 + (s + 1) * 128, :]),
                    owner=h, reads=[C.yn_s[L]], writes=[h])
            for s in range(4):
                transpose_rows(P, C, B, hn[s], hT, s)
            for s_ in range(4):
                xm = xms[s_]
                P.dma("sp", lambda e, t0=t0, xm=xm, s_=s_: e.dma_start(
                    out=xm[:, :], in_=x_src.t.ap()[t0 + s_ * 128:t0 + (s_ + 1) * 128, :]),
                    owner=xm, reads=[x_src], writes=[xm])
            for cb in range(4):
                wb = wbig[ibig % 2]
                ibig += 1
                P.dma("sp", lambda e, wb=wb, cb=cb: e.dma_start(out=wb[:, 0:16, :], in_=wo_view[:, :, cb * 512:(cb + 1) * 512]),
                      owner=wb, reads=[C.wbf[("w_out", L)]], writes=[wb])
                for s in range(4):
                    pp = pa[ipa % 6]
                    ipa += 1
                    for kc in range(16):
                        mm(P, pp[:, :], hT[:, kc, s * 128:(s + 1) * 128], wb[:, kc, :], kc == 0, kc == 15, [hT, wb], [pp], kc == 15)
                    xm = xms[s]
                    P.op("dve", lambda e, pp=pp, xm=xm, cb=cb: e.tensor_tensor(
                        out=xm[:, cb * 512:(cb + 1) * 512], in0=pp[:, :], in1=xm[:, cb * 512:(cb + 1) * 512], op=ALU.add),
                        reads=[pp, xm], writes=[xm])
            for s in range(4):
                h, xm, ssq, sd, rs = hn[s], xms[s], ssqs[s], sds[s], rss[s]
                act(P, h[:, :], xm[:, :], AF.Square, [xm], [h, ssq], accum=ssq[:, 0:1])
                act(P, sd[:, 0:1], ssq[:, 0:1], AF.Ln, [ssq, C.epsb], [sd], scale=1.0 / D, bias=C.epsb[:, 0:1])
                act(P, rs[:, 0:1], sd[:, 0:1], AF.Exp, [sd], [rs], scale=-0.5)
                P.op("dve", lambda e, h=h, xm=xm, rs=rs: e.scalar_tensor_tensor(
                    out=h[:, :], in0=xm[:, :], scalar=rs[:, 0:1], in1=gain_f[:, :], op0=ALU.mult, op1=ALU.mult),
                    reads=[xm, rs, gain_f], writes=[h])
            for s in range(4):
                transpose_rows(P, C, B, hn[s], hT, s)
            for fb in range(NFF // 2):
                wg = wgu[igu % 4]
                wu = wgu[(igu + 1) % 4]
                igu += 2
                P.dma("sp", lambda e, wg=wg, fb=fb: e.dma_start(out=wg[:, :, :], in_=wg_view[:, :, fb * 256:(fb + 1) * 256]),
                      owner=wg, reads=[C.wbf[("w_gate", L)]], writes=[wg])
                P.dma("sp", lambda e, wu=wu, fb=fb: e.dma_start(out=wu[:, :, :], in_=wu_view[:, :, fb * 256:(fb + 1) * 256]),
                      owner=wu, reads=[C.wbf[("w_up", L)]], writes=[wu])
                for half in range(2):
                    fc = fb * 2 + half
                    pg = pa[ipa % 6]
                    pu = pa[(ipa + 1) % 6]
                    ipa += 2
                    for kc in range(16):
                        mm(P, pg[:, :], wg[:, kc, half * 128:(half + 1) * 128], hT[:, kc, :], kc == 0, kc == 15, [hT, wg], [pg], kc == 15)
                    for kc in range(16):
                        mm(P, pu[:, :], wu[:, kc, half * 128:(half + 1) * 128], hT[:, kc, :], kc == 0, kc == 15, [hT, wu], [pu], kc == 15)
                    sgb = sg[fc % 2]
                    act(P, sgb[:, :], pg[:, :], AF.Silu, [pg], [sgb])
                    P.op("dve", lambda e, sgb=sgb, pu=pu, fc=fc: e.tensor_tensor(
                        out=aT[:, fc, :], in0=sgb[:, :], in1=pu[:, :], op=ALU.mult), reads=[sgb, pu], writes=[aT])
            for cb in range(4):
                for half in range(2):
                    wb = wbig[ibig % 2]
                    ibig += 1
                    P.dma("sp", lambda e, wb=wb, cb=cb, half=half: e.dma_start(
                        out=wb[:, :, :], in_=wd_view[:, half * 22:(half + 1) * 22, cb * 512:(cb + 1) * 512]),
                        owner=wb, reads=[C.wbf[("w_down", L)]], writes=[wb])
                    for s in range(4):
                        pp = pa[ipa % 6]
                        ipa += 1
                        for j in range(22):
                            mm(P, pp[:, :], aT[:, half * 22 + j, s * 128:(s + 1) * 128], wb[:, j, :], j == 0, j == 21, [aT, wb], [pp], j == 21)
                        xm = xms[s]
                        P.op("dve", lambda e, pp=pp, xm=xm, cb=cb: e.tensor_tensor(
                            out=xm[:, cb * 512:(cb + 1) * 512], in0=pp[:, :], in1=xm[:, cb * 512:(cb + 1) * 512], op=ALU.add),
                            reads=[pp, xm], writes=[xm])
            for s_ in range(4):
                xm = xms[s_]
                P.dma("act", lambda e, t0=t0, xm=xm, s_=s_: e.dma_start(
                    out=x_dst.t.ap()[t0 + s_ * 128:t0 + (s_ + 1) * 128, :], in_=xm[:, :]),
                    owner=xm, reads=[xm], writes=[x_dst])
    P.barrier()


WEIGHTS = (("w_in", D, DIN), ("w_mem_kv", D, 1024), ("w_out", D, D), ("w_gate", D, DFF), ("w_up", D, DFF),
           ("w_down", DFF, D))


def build(cfg):
    nc = bass.Bass("TRN2", target_bir_lowering=False)
    dbg = cfg.get("debug", ())

    def ein(name, shape, dt=F32):
        return Buf(name, nc.dram_tensor(name, list(shape), dt, kind="ExternalInput"))

    with ExitStack() as es:
        P = Prog(nc, es, same_engine_sync=cfg.get("same", True))
        C = Ctx()
        C.x = ein("x", [S, D])
        C.mem = ein("mem", [256, D])
        C.win = {name: ein(name, [2, r, c]) for name, r, c in WEIGHTS}
        C.gb_d = ein("gb", [128, 8, D])
        C.hg_d = ein("hg", [128, 8])
        C.ident_d = ein("ident", [128, 128], BF16)

        def scratch(name, shape, dt):
            kind = "ExternalOutput" if name in dbg else "Internal"
            return Buf(name, nc.dram_tensor(name, list(shape), dt, kind=kind))

        C.wbf = {(name, L): scratch(f"{name}_bf{L}", [r, c], BF16) for name, r, c in WEIGHTS for L in range(2)}
        C.q_s = [scratch(f"q_s{L}", [1024, S], BF16) for L in range(2)]
        C.k_s = [scratch(f"k_s{L}", [1024, S], BF16) for L in range(2)]
        C.v_s = [scratch(f"v_s{L}", [S, 1024], BF16) for L in range(2)]
        C.u_s = [scratch(f"u_s{L}", [512, S], BF16) for L in range(2)]
        C.m_s = [scratch(f"m_s{L}", [512, S], BF16) for L in range(2)]
        C.yn_s = [scratch(f"yn_s{L}", [S, D], BF16) for L in range(2)]
        C.x1 = scratch("x1_s", [S, D], F32)
        C.out = Buf("out", nc.dram_tensor("out", [S // 2, D], F32, kind="ExternalOutput"))
        C.tz_d = ein("tz", [2, 128, 7, 8, 128])
        C.mk_d = ein("mk", [128, 63, 128], BF16)
        C.dft_d = ein("dft", [64, 128], BF16)
        C.mb_d = ein("mb", [128, 64, 2, 2, 128], BF16)
        C.ccsc_d = ein("ccsc", [128, 2, 128])
        C.wf_d = ein("wf", [2, 4, 128, 128])

        C.hg = P.sbuf("hg_sb", [128, 8], F32)
        C.ident = P.sbuf("ident_sb", [128, 128], BF16)
        C.ones = P.sbuf("ones_sb", [128, 128], BF16)
        C.epsb = P.sbuf("epsb_sb", [128, 1], F32)
        P.dma("sp", lambda e: e.dma_start(out=C.hg[:, :], in_=C.hg_d.t.ap()), owner=C.hg, writes=[C.hg])
        P.dma("sp", lambda e: e.dma_start(out=C.ident[:, :], in_=C.ident_d.t.ap()), owner=C.ident, writes=[C.ident])
        P.op("dve", lambda e: e.memset(C.ones[:, :], 1.0), writes=[C.ones])
        P.op("dve", lambda e: e.memset(C.epsb[:, :], EPS), writes=[C.epsb])
        for col in (0, 2, 4, 6):
            P.op("dve", lambda e, col=col: e.tensor_scalar(out=C.hg[:, col:col + 1], in0=C.hg[:, col:col + 1],
                                                            scalar1=128.0 ** -0.5, scalar2=None, op0=ALU.mult),
                 reads=[C.hg], writes=[C.hg])

        cvt_owner = Buf("cvt")
        P._dsem(cvt_owner)
        P.bg_sems = {cvt_owner.dsem}
        wanted = cfg.get("weights", [(n, L) for L in range(2) for n, _, _ in WEIGHTS])

        def convert(names, L_sel):
            for (name, L) in wanted:
                if L != L_sel or name not in names:
                    continue
                rows = dict((n, r) for n, r, c in WEIGHTS)[name]
                src = C.win[name]
                dst = C.wbf[(name, L)]
                for r0 in range(0, rows, 256):
                    P.dma("pool", lambda e, src=src, dst=dst, L=L, r0=r0: e.dma_start(
                        out=dst.t.ap()[r0:r0 + 256, :], in_=src.t.ap()[L, r0:r0 + 256, :]),
                        owner=cvt_owner, reads=[src], writes=[dst])
        convert(("w_in", "w_mem_kv", "w_out"), 0)
        stage = {"n": 0}

        for ph in cfg["phases"]:
            if ph[0] == "A":
                _, L, tiles = ph
                phase_A(P, C, L, C.x if L == 0 else C.x1, tiles)
                if stage["n"] == 0:
                    convert(("w_gate", "w_up", "w_down"), 0)
                    convert(tuple(n for n, _, _ in WEIGHTS), 1)
                    stage["n"] = 2
            elif ph[0] == "X":
                phase_X(P, C, ph[1], ph[2])
            elif ph[0] == "N":
                phase_N(P, C, ph[1], ph[2])
            elif ph[0] == "F":
                phase_F(P, C, ph[1], ph[2])
                if stage["n"] == 1:
                    convert(tuple(n for n, _, _ in WEIGHTS), 1)
                    stage["n"] = 2
            elif ph[0] == "O":
                L = ph[1]
                phase_O(P, C, L, ph[2], C.x if L == 0 else C.x1, C.x1 if L == 0 else C.out)

        if stage["n"] < 2:
            if stage["n"] == 0:
                convert(("w_gate", "w_up", "w_down"), 0)
            convert(tuple(n for n, _, _ in WEIGHTS), 1)
        P.barrier(final=True)
        stuck = P.check()
        if stuck:
            raise RuntimeError(f"sync graph deadlock: {stuck}")
        P.emit()
    return nc


def host_inputs(inputs, core):
    b, hf = core // 2, core % 2
    f32 = np.float32
    x = np.asarray(inputs["x"][b])
    if hf:
        x = np.concatenate([x[S // 2:], x[:S // 2]], 0)
    m = {"x": np.ascontiguousarray(x, dtype=f32), "mem": np.ascontiguousarray(inputs["mem"][b], dtype=f32)}
    for name, _, _ in WEIGHTS:
        m[name] = np.ascontiguousarray(inputs[name], dtype=f32)
    gb = np.stack([inputs[k][L] for L in range(2) for k in ("attn_norm", "mem_norm", "out_norm", "ffn_norm")], 0)
    m["gb"] = np.ascontiguousarray(np.broadcast_to(gb[None], (128, 8, D)), dtype=f32)
    hg = np.stack([inputs[k][L] for L in range(2) for k in ("na_q_norm", "na_k_norm", "mem_q_norm", "mem_k_norm")], 1)
    m["hg"] = np.ascontiguousarray(hg, dtype=f32)
    m["ident"] = np.eye(128, dtype=f32).astype(ml_dtypes.bfloat16)
    return m


def host_tables(inputs, hf):
    f32 = np.float32
    bf = ml_dtypes.bfloat16
    rpb = np.asarray(inputs["na_rpb"], dtype=f32)
    kc = np.arange(64)[:, None]
    qc = np.arange(64)[None, :]
    cstart = np.clip(qc - 8, 0, 48)
    colmask = (kc >= cstart) & (kc < cstart + 16)
    colidx = np.clip(kc - qc + 15, 0, 30)
    tz = np.full((2, 128, 7, 8, 128), NEG, dtype=f32)
    for L in range(2):
        for dl in range(-3, 4):
            for a in range(2):
                for r in range(2):
                    d = 2 * dl + a - r
                    if abs(d) > 7:
                        continue
                    blk = rpb[L][:, d + 7, :][:, colidx]
                    blk = np.where(colmask[None], blk, f32(NEG))
                    tz[L, a * 64:(a + 1) * 64, dl + 3, :, r * 64:(r + 1) * 64] = blk.transpose(1, 0, 2)
    mk = np.full((128, 63, 128), NEG, dtype=f32)
    inv = {v: k for k, v in SPECIAL.items()}
    for cls in range(9):
        for dl in range(-3, 4):
            for a in range(2):
                for r in range(2):
                    if cls == 0:
                        d = 2 * dl + a - r
                        ok = -4 <= d <= 3
                    else:
                        bp = inv[cls]
                        rq = (2 * bp + r + 64 * hf) % 128
                        rk = (2 * ((bp + dl) % 64) + a + 64 * hf) % 128
                        st = min(max(rq - 4, 0), 120)
                        ok = st <= rk <= st + 7
                    if ok:
                        mk[a * 64:(a + 1) * 64, cls * 7 + dl + 3, r * 64:(r + 1) * 64] = 0.0
    n1 = np.arange(64, dtype=np.float64)[:, None]
    k1 = np.arange(64, dtype=np.float64)[None, :]
    ang = 2 * np.pi * n1 * k1 / 64
    dft = np.concatenate([np.cos(ang), -np.sin(ang)], 1)
    n2 = np.arange(128, dtype=np.float64)[:, None, None]
    k1 = np.arange(64, dtype=np.float64)[None, :, None]
    k2 = np.arange(128, dtype=np.float64)[None, None, :]
    ang = 2 * np.pi * n2 * (k1 + 64 * k2) / S
    sign = np.where(((n2 + k1) % 2) == 1, -1.0, 1.0) if hf else 1.0
    Mr = np.cos(ang) * sign
    Mi = -np.sin(ang) * sign
    mb = np.stack([np.stack([Mr, Mi], 2), np.stack([-Mi, Mr], 2)], 2)
    c = np.arange(128, dtype=np.float64)
    angc = 2 * np.pi * np.outer(c, c) / 128
    ccsc = np.stack([np.cos(angc), np.sin(angc)], 1)
    return {"tz": tz, "mk": mk.astype(bf), "dft": dft.astype(f32).astype(bf), "mb": mb.astype(f32).astype(bf),
            "ccsc": ccsc.astype(f32), "wf": np.ascontiguousarray(inputs["w_fourier"], dtype=f32)}


ALL = "qkvum"
FULL_CFG = {
    "phases": [
        ("A", 0, [(i, ALL) for i in range(16)]),
        ("X", 0, list(range(16))),
        ("N", 0, list(range(16))),
        ("F", 0, 128),
        ("O", 0, list(range(16))),
        ("A", 1, [(i, ALL) for i in range(8)] + [(8, "kvu")] + [(i, "u") for i in range(9, 15)] + [(15, "kvu")]),
        ("X", 1, list(range(8))),
        ("N", 1, list(range(8))),
        ("F", 1, 64),
        ("O", 1, list(range(8))),
    ],
}
_NC = {}


def kernel(**inputs):
    inputs = {k: np.asarray(v) for k, v in inputs.items()}
    if "nc" not in _NC:
        _NC["nc"] = build(FULL_CFG)
    tabs = [host_tables(inputs, hf) for hf in range(2)]
    in_maps = []
    for core in range(8):
        m = host_inputs(inputs, core)
        m.update(tabs[core % 2])
        in_maps.append(m)
    res = run_bass_kernel_spmd(_NC["nc"], in_maps, core_ids=list(range(8)))
    out = np.empty((4, S, D), dtype=np.float32)
    for core in range(8):
        b, hf = core // 2, core % 2
        out[b, hf * (S // 2):(hf + 1) * (S // 2)] = res.results[core]["out"]
    return out
```

```python
from contextlib import ExitStack

import numpy as np
import ml_dtypes

import concourse.bass as bass
import concourse.mybir as mybir
from concourse.bass_utils import run_bass_kernel_spmd

F32 = mybir.dt.float32
BF16 = mybir.dt.bfloat16
AF = mybir.ActivationFunctionType
ALU = mybir.AluOpType

S = 8192
D = 2048
DIN = 4096
DFF = 5632
NFF = DFF // 128
T = 512
EPS = 1e-6
NEG = -30000.0
ENGS = ("pe", "act", "dve", "pool", "sp")


class Buf:
    __slots__ = ("name", "t", "w", "r", "dsem", "dcnt", "is_psum")

    def __init__(self, name, t=None):
        self.name = name
        self.t = t
        self.is_psum = False
        self.w = {}
        self.r = {}
        self.dsem = None
        self.dcnt = 0

    def __getitem__(self, k):
        return self.t[k]


class Prog:
    def __init__(self, nc, es, same_engine_sync=True):
        self.nc = nc
        self.es = es
        self.same = same_engine_sync
        self.esem = {e: es.enter_context(nc.semaphore("sem_" + e)) for e in ENGS}
        self.ecnt = {e: 0 for e in ENGS}
        self.q = {e: [] for e in ENGS}
        self.meta = {e: [] for e in ENGS}
        self.seen = {e: {} for e in ENGS}
        self.dcur = {}
        self.sem_pool = []
        self.scoped = []
        self.ninst = 0

    def sbuf(self, name, shape, dtype, es=None):
        self.uid = getattr(self, "uid", 0) + 1
        name = f"{name}_{self.uid}"
        t = (es or self.es).enter_context(self.nc.sbuf_tensor(name, list(shape), dtype))
        b = Buf(name, t)
        if es is not None:
            self.scoped.append(b)
        return b

    def psum(self, name, shape, dtype, es=None):
        self.uid = getattr(self, "uid", 0) + 1
        name = f"{name}_{self.uid}"
        t = (es or self.es).enter_context(self.nc.psum_tensor(name, list(shape), dtype))
        b = Buf(name, t)
        b.is_psum = True
        return b

    def dram(self, name, shape, dtype, kind="Internal"):
        t = self.nc.dram_tensor(name, list(shape), dtype, kind=kind)
        return Buf(name, t)

    def _dsem(self, b):
        if b.dsem is None:
            if self.sem_pool:
                b.dsem = self.sem_pool.pop()
                b.dcnt = self.dcur[b.dsem]
            else:
                b.dsem = self.es.enter_context(self.nc.semaphore("dsem_" + b.name))
                self.dcur[b.dsem] = 0
        return b.dsem

    def _waits(self, e, reads, writes, skip_own=None):
        deps = {}
        own_sem = self.esem[e]
        for b in reads:
            for s, v in b.w.items():
                if deps.get(s, 0) < v:
                    deps[s] = v
            if b.is_psum:
                for s, v in b.r.items():
                    if s is not own_sem and deps.get(s, 0) < v:
                        deps[s] = v
        for b in writes:
            for d in (b.w, b.r):
                for s, v in d.items():
                    if deps.get(s, 0) < v:
                        deps[s] = v
        seen = self.seen[e]
        own = self.esem[e]
        if skip_own is None:
            skip_own = (e == "pe") or not self.same
        waits = []
        for s, v in deps.items():
            if s is own and skip_own:
                continue
            if seen.get(s, 0) >= v:
                continue
            seen[s] = v
            waits.append((s, v))
        return waits

    def op(self, e, fn, reads=(), writes=(), inc=True):
        waits = self._waits(e, reads, writes)
        own = self.esem[e]
        cnt = self.ecnt[e] + 1
        if inc:
            self.ecnt[e] = cnt

        def emit(eng, waits=waits, fn=fn, inc=inc, own=own):
            for s, v in waits:
                eng.wait_ge(s, v)
            ins = fn(eng)
            if inc:
                ins.then_inc(own, 1)
        self.q[e].append(emit)
        self.meta[e].append((waits, (own, 1) if inc else None))
        self.ninst += 1
        for b in reads:
            if b.r.get(own, 0) < cnt:
                b.r[own] = cnt
        for b in writes:
            b.w = {own: cnt}
            b.r = {}

    def dma(self, qe, fn, owner, reads=(), writes=()):
        ds = self._dsem(owner)
        waits = self._waits(qe, reads, writes, skip_own=False)
        seen = self.seen[qe]
        if owner.dcnt > 0 and seen.get(ds, 0) < owner.dcnt:
            seen[ds] = owner.dcnt
            waits.append((ds, owner.dcnt))
        owner.dcnt += 16
        val = owner.dcnt
        self.dcur[ds] = val

        def emit(eng, waits=waits, fn=fn, ds=ds):
            for s, v in waits:
                eng.wait_ge(s, v)
            fn(eng).then_inc(ds, 16)
        self.q[qe].append(emit)
        self.meta[qe].append((waits, (ds, 16)))
        self.ninst += 1
        for b in reads:
            if b.r.get(ds, 0) < val:
                b.r[ds] = val
        for b in writes:
            b.w = {ds: val}
            b.r = {}

    def barrier(self, final=False):
        targets = [(self.esem[e], self.ecnt[e]) for e in ENGS if self.ecnt[e] > 0]
        skip = getattr(self, "bg_sems", ())
        targets += [(sm, v) for sm, v in self.dcur.items() if v > 0 and (final or sm not in skip)]
        for e in ENGS:
            seen = self.seen[e]
            waits = []
            for s, v in targets:
                if seen.get(s, 0) < v:
                    seen[s] = v
                    waits.append((s, v))

            def emit(eng, waits=waits):
                for s, v in waits:
                    eng.wait_ge(s, v)
            self.q[e].append(emit)
            self.meta[e].append((waits, None))
        for b in self.scoped:
            if b.dsem is not None:
                self.sem_pool.append(b.dsem)
                b.dsem = None
        self.scoped = []

    def check(self):
        val = {}
        pos = {e: 0 for e in ENGS}
        progress = True
        while progress:
            progress = False
            for e in ENGS:
                m = self.meta[e]
                while pos[e] < len(m):
                    waits, inc = m[pos[e]]
                    if any(val.get(id(s), 0) < v for s, v in waits):
                        break
                    if inc is not None:
                        val[id(inc[0])] = val.get(id(inc[0]), 0) + inc[1]
                    pos[e] += 1
                    progress = True
        stuck = {e: (pos[e], len(self.meta[e])) for e in ENGS if pos[e] < len(self.meta[e])}
        return stuck or None

    def emit(self):
        with self.nc.Block() as block:
            for e, deco in (("sp", block.sync), ("act", block.scalar), ("pe", block.tensor),
                            ("dve", block.vector), ("pool", block.gpsimd)):
                lst = self.q[e]
                if not lst:
                    continue

                def body(eng, lst=lst):
                    for f in lst:
                        f(eng)
                deco(body)


def mm(P, out_ap, lhsT, rhs, start, stop, reads, writes, inc):
    P.op("pe", lambda e: e.matmul(out_ap, lhsT, rhs, start=start, stop=stop),
         reads=reads, writes=writes, inc=inc)


def act(P, out_ap, in_ap, func, reads, writes, scale=None, bias=None, accum=None):
    kw = {}
    if scale is not None:
        kw["scale"] = scale
    if bias is not None:
        kw["bias"] = bias
    if accum is not None:
        kw["accum_out"] = accum
    P.op("act", lambda e: e.activation(out=out_ap, in_=in_ap, func=func, **kw), reads=reads, writes=writes)


class Ctx:
    pass


def rstd_from_ssq(P, C, ssq_ap, n, out_ap, ssq_buf, tmp_buf, out_buf):
    act(P, tmp_buf_ap(tmp_buf, ssq_ap), ssq_ap, AF.Sqrt, [ssq_buf, C.epsb], [tmp_buf],
        scale=1.0 / n, bias=C.epsb[0:ssq_ap.shape[0], 0:1])
    P.op("dve", lambda e: e.reciprocal(out=out_ap, in_=tmp_buf_ap(tmp_buf, ssq_ap)),
         reads=[tmp_buf], writes=[out_buf])


def tmp_buf_ap(tmp_buf, like):
    sh = like.shape
    if len(sh) == 2:
        return tmp_buf[0:sh[0], 0:sh[1]]
    return tmp_buf[0:sh[0], 0:sh[1], 0:sh[2]]


def norm_transpose_tile(P, C, es_bufs, src_ap_fn, gain, hT, nsub):
    B = es_bufs
    for s in range(nsub):
        xt = B.xt[s % 2]
        P.dma("sp", lambda e, xt=xt, s=s: e.dma_start(out=xt[:, :], in_=src_ap_fn(s)), owner=xt, writes=[xt])
        act(P, B.junk[:, :], xt[:, :], AF.Square, [xt], [B.junk, B.ssq], accum=B.ssq[:, 0:1])
        act(P, B.sd[:, 0:1], B.ssq[:, 0:1], AF.Ln, [B.ssq, C.epsb], [B.sd], scale=1.0 / D, bias=C.epsb[:, 0:1])
        act(P, B.rs[:, 0:1], B.sd[:, 0:1], AF.Exp, [B.sd], [B.rs], scale=-0.5)
        hn = B.hn[s % 2]
        P.op("dve", lambda e, xt=xt, hn=hn: e.scalar_tensor_tensor(
            out=hn[:, :], in0=xt[:, :], scalar=B.rs[:, 0:1], in1=gain[:, :], op0=ALU.mult, op1=ALU.mult),
            reads=[xt, B.rs, gain], writes=[hn])
        transpose_rows(P, C, B, hn, hT, s)


def transpose_rows(P, C, B, hn, hT, s):
    for j in range(2):
        tp = B.tp[j]
        for i in range(8):
            kc = j * 8 + i
            P.op("pe", lambda e, tp=tp, i=i, kc=kc, hn=hn: e.transpose(
                out=tp[:, i, :], in_=hn[:, kc * 128:(kc + 1) * 128], identity=C.ident[:, :]),
                reads=[hn, C.ident], writes=[tp], inc=(i == 7))
        if j == 0:
            act(P, hT[:, 0:8, s * 128:(s + 1) * 128], tp[:, :, :], AF.Copy, [tp], [hT])
        else:
            P.op("dve", lambda e, tp=tp: e.tensor_copy(out=hT[:, 8:16, s * 128:(s + 1) * 128], in_=tp[:, :, :]),
                 reads=[tp], writes=[hT])


def load_gain(P, C, es, pfx, idx):
    g = P.sbuf(f"{pfx}_gain", [128, D], F32, es)
    P.dma("sp", lambda e: e.dma_start(out=g[:, :], in_=C.gb_d.t.ap()[:, idx, :]), owner=g, writes=[g])
    return g


def alloc_norm_bufs(P, es, pfx):
    B = Ctx()
    B.xt = [P.sbuf(f"{pfx}_xt{i}", [128, D], F32, es) for i in range(2)]
    B.hn = [P.sbuf(f"{pfx}_hn{i}", [128, D], BF16, es) for i in range(2)]
    B.junk = P.sbuf(f"{pfx}_junk", [128, D], BF16, es)
    B.ssq = P.sbuf(f"{pfx}_ssq", [128, 1], F32, es)
    B.sd = P.sbuf(f"{pfx}_sd", [128, 1], F32, es)
    B.rs = P.sbuf(f"{pfx}_rs", [128, 1], F32, es)
    B.tp = [P.psum(f"{pfx}_tp{i}", [128, 8, 128], BF16, es) for i in range(2)]
    return B


def head_norm_fm(P, C, A, pm, gcol, dst_ap, dst_buf):
    act(P, A.sqb[:, :], pm[:, :], AF.Square, [pm], [A.sqb])
    ps = A.ps[A.psi % 2]
    A.psi += 1
    mm(P, ps[:, :], C.ones[:, :], A.sqb[:, :], True, True, [C.ones, A.sqb], [ps], True)
    act(P, A.sd[:, :], ps[:, :], AF.Sqrt, [ps, C.epsb], [A.sd], scale=1.0 / 128, bias=C.epsb[:, 0:1])
    P.op("dve", lambda e: e.reciprocal(out=A.rinv[:, :], in_=A.sd[:, :]), reads=[A.sd], writes=[A.rinv])
    P.op("dve", lambda e: e.scalar_tensor_tensor(out=dst_ap, in0=pm[:, :], scalar=gcol, in1=A.rinv[:, :],
                                                  op0=ALU.mult, op1=ALU.mult),
         reads=[pm, A.rinv, C.hg], writes=[dst_buf])


def phase_A(P, C, L, x_src, tiles):
    with ExitStack() as es:
        B = alloc_norm_bufs(P, es, "A")
        A = Ctx()
        A.hT = [P.sbuf(f"A_hT{i}", [128, 16, T], BF16, es) for i in range(2)]
        A.wt = [P.sbuf(f"A_wt{i}", [128, 16, 256], BF16, es) for i in range(3)]
        A.sqb = [P.sbuf(f"A_sqb{i}", [128, T], BF16, es) for i in range(2)]
        A.sd = [P.sbuf(f"A_sdw{i}", [128, T], F32, es) for i in range(2)]
        A.rinv = [P.sbuf(f"A_rinv{i}", [128, T], F32, es) for i in range(2)]
        A.ob = [P.sbuf(f"A_ob{i}", [128, T], BF16, es) for i in range(3)]
        A.vb = [P.sbuf(f"A_vb{i}", [128, 4, 256], BF16, es) for i in range(2)]
        A.pm = [P.psum(f"A_pm{i}", [128, T], F32, es) for i in range(3)]
        A.pv = P.psum("A_pv", [128, 2, 256], F32, es)
        A.ps = [P.psum(f"A_ps{i}", [128, T], F32, es) for i in range(2)]
        w_bf = C.wbf[("w_in", L)]
        w_view = w_bf.t.ap().rearrange("(kc p) n -> p kc n", p=128)
        gain = load_gain(P, C, es, "A", L * 4 + 0)

        xt4 = B.xt + [P.sbuf(f"A_xtx{i}", [128, D], F32, es) for i in range(2)]
        hn4 = B.hn + [P.sbuf(f"A_hnx{i}", [128, D], BF16, es) for i in range(2)]

        def pro_norm(it):
            ti, _ = tiles[it]
            t0 = ti * T
            for s in range(4):
                xt, hn = xt4[s], hn4[s]
                P.dma("sp", lambda e, xt=xt, s=s: e.dma_start(out=xt[:, :], in_=x_src.t.ap()[t0 + s * 128:t0 + (s + 1) * 128, :]),
                      owner=xt, reads=[x_src], writes=[xt])
                act(P, B.junk[:, :], xt[:, :], AF.Square, [xt], [B.junk, B.ssq], accum=B.ssq[:, 0:1])
                act(P, B.sd[:, 0:1], B.ssq[:, 0:1], AF.Ln, [B.ssq, C.epsb], [B.sd], scale=1.0 / D, bias=C.epsb[:, 0:1])
                act(P, B.rs[:, 0:1], B.sd[:, 0:1], AF.Exp, [B.sd], [B.rs], scale=-0.5)
                P.op("dve", lambda e, xt=xt, hn=hn: e.scalar_tensor_tensor(
                    out=hn[:, :], in0=xt[:, :], scalar=B.rs[:, 0:1], in1=gain[:, :], op0=ALU.mult, op1=ALU.mult),
                    reads=[xt, B.rs, gain], writes=[hn])

        def pro_tr(it):
            for s in range(4):
                transpose_rows(P, C, B, hn4[s], A.hT[it % 2], s)

        units = []
        cnt = {"w": 0, "pm": 0, "ob": 0, "vb": 0, "u": 0}
        for it, (ti, kinds) in enumerate(tiles):
            t0 = ti * T
            hT = A.hT[it % 2]
            first = len(units)
            for blk in range(16):
                kind = "qqqqkkkkvvvvuumm"[blk]
                if kind not in kinds:
                    continue
                wt = A.wt[cnt["w"] % 3]
                cnt["w"] += 1

                def load_w(wt=wt, blk=blk):
                    P.dma("sp", lambda e: e.dma_start(out=wt[:, :, :], in_=w_view[:, :, blk * 256:(blk + 1) * 256]),
                          owner=wt, reads=[w_bf], writes=[wt])
                if kind == "v":
                    vb = A.vb[cnt["vb"] % 2]
                    cnt["vb"] += 1

                    def s1(load_w=load_w, vb=vb, hT=hT, wt=wt, blk=blk, t0=t0):
                        load_w()
                        pv = A.pv
                        for s in range(4):
                            for kc in range(16):
                                mm(P, pv[:, s % 2, :], hT[:, kc, s * 128:(s + 1) * 128], wt[:, kc, :], kc == 0, kc == 15,
                                   [hT, wt], [pv], kc == 15)
                            if s % 2 == 0:
                                act(P, vb[:, s, :], pv[:, s % 2, :], AF.Copy, [pv], [vb])
                            else:
                                P.op("dve", lambda e, s=s: e.tensor_copy(out=vb[:, s, :], in_=pv[:, s % 2, :]),
                                     reads=[pv], writes=[vb])
                        c0 = (blk - 8) * 256
                        P.dma("act", lambda e: e.dma_start(
                            out=C.v_s[L].t.ap()[t0:t0 + T, c0:c0 + 256].rearrange("(s p) n -> p s n", p=128), in_=vb[:, :, :]),
                            owner=vb, reads=[vb], writes=[C.v_s[L]])
                    units.append([s1, None, None])
                    continue
                for half in range(2):
                    cc = blk * 2 + half
                    pm = A.pm[cnt["pm"] % 3]
                    cnt["pm"] += 1
                    ob = A.ob[cnt["ob"] % 3]
                    cnt["ob"] += 1
                    ui = cnt["u"]
                    cnt["u"] += 1
                    if kind == "u":
                        dst, r0, gcol = C.u_s[L], (cc - 24) * 128, None
                    elif kind == "q":
                        dst, r0, gcol = C.q_s[L], cc * 128, L * 4 + 0
                    elif kind == "k":
                        dst, r0, gcol = C.k_s[L], (cc - 8) * 128, L * 4 + 1
                    else:
                        dst, r0, gcol = C.m_s[L], (cc - 28) * 128, L * 4 + 2

                    def s1(load_w=load_w if half == 0 else None, pm=pm, hT=hT, wt=wt, half=half):
                        if load_w is not None:
                            load_w()
                        for kc in range(16):
                            mm(P, pm[:, :], wt[:, kc, half * 128:(half + 1) * 128], hT[:, kc, :], kc == 0, kc == 15,
                               [hT, wt], [pm], kc == 15)

                    def s2(pm=pm, ob=ob, gcol=gcol, ui=ui):
                        if gcol is None:
                            act(P, ob[:, :], pm[:, :], AF.Copy, [pm], [ob])
                            return
                        sqb, ps = A.sqb[ui % 2], A.ps[ui % 2]
                        act(P, sqb[:, :], pm[:, :], AF.Square, [pm], [sqb])
                        mm(P, ps[:, :], C.ones[:, :], sqb[:, :], True, True, [C.ones, sqb], [ps], True)

                    def s3(pm=pm, ob=ob, gcol=gcol, ui=ui, dst=dst, r0=r0, t0=t0):
                        if gcol is not None:
                            ps, sd, rinv = A.ps[ui % 2], A.sd[ui % 2], A.rinv[ui % 2]
                            act(P, sd[:, :], ps[:, :], AF.Ln, [ps, C.epsb], [sd], scale=1.0 / 128, bias=C.epsb[:, 0:1])
                            act(P, rinv[:, :], sd[:, :], AF.Exp, [sd], [rinv], scale=-0.5)
                            P.op("dve", lambda e: e.scalar_tensor_tensor(
                                out=ob[:, :], in0=pm[:, :], scalar=C.hg[:, gcol:gcol + 1], in1=rinv[:, :],
                                op0=ALU.mult, op1=ALU.mult), reads=[pm, rinv, C.hg], writes=[ob])
                        P.dma("act", lambda e: e.dma_start(out=dst.t.ap()[r0:r0 + 128, t0:t0 + T], in_=ob[:, :]),
                              owner=ob, reads=[ob], writes=[dst])
                    units.append([s1, s2, s3])
            if it + 1 < len(tiles):
                n = len(units) - first
                for frac, fn in ((n // 4, pro_norm), ((3 * n) // 4, pro_tr)):
                    pos = min(first + frac, len(units) - 1)
                    old_fn = units[pos][0]

                    def s1_with_pro(old_fn=old_fn, fn=fn, nit=it + 1):
                        fn(nit)
                        old_fn()
                    units[pos][0] = s1_with_pro
        pro_norm(0)
        pro_tr(0)
        run_pipeline(units, 3)
    P.barrier()


def run_pipeline(units, nstages):
    n = len(units)
    for step in range(n + nstages - 1):
        for st in range(nstages):
            u = step - st
            if 0 <= u < n and units[u][st] is not None:
                units[u][st]()


def fin_a(P, C, Fz, po, po_buf, has_den, n=128):
    if has_den:
        if has_den == "copy":
            P.op("dve", lambda e: e.tensor_copy(out=Fz.yb[0:n, :], in_=po[0:n, 0:128]), reads=[po_buf], writes=[Fz.yb])
        else:
            P.op("dve", lambda e: e.reciprocal(out=Fz.rden[0:n, 0:1], in_=po[0:n, 128:129]), reads=[po_buf], writes=[Fz.rden])
            P.op("dve", lambda e: e.tensor_scalar(out=Fz.yb[0:n, :], in0=po[0:n, 0:128], scalar1=Fz.rden[0:n, 0:1],
                                                   scalar2=None, op0=ALU.mult), reads=[po_buf, Fz.rden], writes=[Fz.yb])
        src, src_buf = Fz.yb[0:n, :], Fz.yb
        P.op("dve", lambda e: e.scalar_tensor_tensor(out=Fz.junk[0:n, :], in0=src, scalar=1.0, in1=src, op0=ALU.mult,
                                                      op1=ALU.mult, accum_out=Fz.ssq[0:n, 0:1]),
             reads=[src_buf], writes=[Fz.junk, Fz.ssq])
    else:
        src, src_buf = po[0:n, 0:128], po_buf
        act(P, Fz.junk[0:n, :], src, AF.Square, [src_buf], [Fz.junk, Fz.ssq], accum=Fz.ssq[0:n, 0:1])
    act(P, Fz.sd[0:n, 0:1], Fz.ssq[0:n, 0:1], AF.Ln, [Fz.ssq, C.epsb], [Fz.sd], scale=1.0 / 128, bias=C.epsb[0:n, 0:1])
    act(P, Fz.rs[0:n, 0:1], Fz.sd[0:n, 0:1], AF.Exp, [Fz.sd], [Fz.rs], scale=-0.5)


def fin_b(P, C, Fz, po, po_buf, has_den, gain_ap, gain_buf, dst_ap, dst_buf, n=128):
    if has_den:
        src, src_buf = Fz.yb[0:n, :], Fz.yb
    else:
        src, src_buf = po[0:n, 0:128], po_buf
    P.op("dve", lambda e: e.scalar_tensor_tensor(out=dst_ap, in0=src, scalar=Fz.rs[0:n, 0:1], in1=gain_ap,
                                                  op0=ALU.mult, op1=ALU.mult),
         reads=[src_buf, Fz.rs, gain_buf], writes=[dst_buf])


def finalize_head(P, C, Fz, po, po_buf, has_den, gain_ap, gain_buf, dst_ap, dst_buf, npart=128):
    n = npart
    if has_den:
        P.op("dve", lambda e: e.reciprocal(out=Fz.rden[0:n, 0:1], in_=po[0:n, 128:129]), reads=[po_buf], writes=[Fz.rden])
        P.op("dve", lambda e: e.tensor_scalar(out=Fz.yb[0:n, :], in0=po[0:n, 0:128], scalar1=Fz.rden[0:n, 0:1], scalar2=None,
                                               op0=ALU.mult), reads=[po_buf, Fz.rden], writes=[Fz.yb])
        src, src_buf = Fz.yb[0:n, :], Fz.yb
    else:
        src, src_buf = po[0:n, 0:128], po_buf
    act(P, Fz.junk[0:n, :], src, AF.Square, [src_buf], [Fz.junk, Fz.ssq], accum=Fz.ssq[0:n, 0:1])
    act(P, Fz.sd[0:n, 0:1], Fz.ssq[0:n, 0:1], AF.Ln, [Fz.ssq, C.epsb], [Fz.sd], scale=1.0 / 128, bias=C.epsb[0:n, 0:1])
    act(P, Fz.rs[0:n, 0:1], Fz.sd[0:n, 0:1], AF.Exp, [Fz.sd], [Fz.rs], scale=-0.5)
    P.op("dve", lambda e: e.scalar_tensor_tensor(out=dst_ap, in0=src, scalar=Fz.rs[0:n, 0:1], in1=gain_ap,
                                                  op0=ALU.mult, op1=ALU.mult),
         reads=[src_buf, Fz.rs, gain_buf], writes=[dst_buf])


def alloc_finalize(P, es, pfx):
    Fz = Ctx()
    Fz.rden = P.sbuf(f"{pfx}_rden", [128, 1], F32, es)
    Fz.yb = P.sbuf(f"{pfx}_yb", [128, 128], F32, es)
    Fz.junk = P.sbuf(f"{pfx}_fjunk", [128, 128], BF16, es)
    Fz.ssq = P.sbuf(f"{pfx}_fssq", [128, 1], F32, es)
    Fz.sd = P.sbuf(f"{pfx}_fsd", [128, 1], F32, es)
    Fz.rs = P.sbuf(f"{pfx}_frs", [128, 1], F32, es)
    return Fz


def phase_X(P, C, L, blocks):
    with ExitStack() as es:
        gain_o = load_gain(P, C, es, "Xo", L * 4 + 2)
        kmT = P.sbuf("X_kmT", [128, 4, 256], BF16, es)
        vme = P.sbuf("X_vme", [128, 2, 4, 129], BF16, es)
        qm = [P.sbuf(f"X_qm{i}", [128, 4, T], BF16, es) for i in range(2)]
        pT = [P.sbuf(f"X_pT{i}", [128, 2, T], BF16, es) for i in range(3)]
        ynt = [P.sbuf(f"X_ynt{i}", [128, 4, 512], BF16, es) for i in range(2)]
        FZ = alloc_finalize_sets(P, es, "X", 8)
        P.op("dve", lambda e: e.memset(vme[:, :, :, :], 1.0), writes=[vme])
        with ExitStack() as es2:
            B = alloc_norm_bufs(P, es2, "X")
            gain_m = load_gain(P, C, es2, "Xm", L * 4 + 1)
            hTm = P.sbuf("X_hTm", [128, 16, 256], BF16, es2)
            wts = [P.sbuf(f"X_wt{i}", [128, 16, 256], BF16, es2) for i in range(2)]
            sqb = P.sbuf("X_sqb", [128, 256], BF16, es2)
            sdw = P.sbuf("X_sdw", [128, 256], F32, es2)
            rinv = P.sbuf("X_rinv", [128, 256], F32, es2)
            ps = P.psum("X_ps", [128, T], F32, es2)
            pmk = [P.psum(f"X_pmk{i}", [128, T], F32, es2) for i in range(3)]
            norm_transpose_tile(P, C, B, lambda s: C.mem.t.ap()[s * 128:(s + 1) * 128, :], gain_m, hTm, 2)
            w_bf = C.wbf[("w_mem_kv", L)]
            w_view = w_bf.t.ap().rearrange("(kc p) n -> p kc n", p=128)
            pmi = 0
            for blk in range(4):
                wt = wts[blk % 2]
                P.dma("sp", lambda e, wt=wt, blk=blk: e.dma_start(out=wt[:, :, :], in_=w_view[:, :, blk * 256:(blk + 1) * 256]),
                      owner=wt, reads=[w_bf], writes=[wt])
                if blk < 2:
                    for half in range(2):
                        h = blk * 2 + half
                        pm = pmk[pmi % 3]
                        pmi += 1
                        for kc in range(16):
                            mm(P, pm[:, 0:256], wt[:, kc, half * 128:(half + 1) * 128], hTm[:, kc, :], kc == 0, kc == 15,
                               [hTm, wt], [pm], kc == 15)
                        act(P, sqb[:, :], pm[:, 0:256], AF.Square, [pm], [sqb])
                        mm(P, ps[:, 0:256], C.ones[:, :], sqb[:, :], True, True, [C.ones, sqb], [ps], True)
                        act(P, sdw[:, :], ps[:, 0:256], AF.Ln, [ps, C.epsb], [sdw], scale=1.0 / 128, bias=C.epsb[:, 0:1])
                        act(P, rinv[:, :], sdw[:, :], AF.Exp, [sdw], [rinv], scale=-0.5)
                        P.op("dve", lambda e, pm=pm, h=h: e.scalar_tensor_tensor(
                            out=kmT[:, h, :], in0=pm[:, 0:256], scalar=C.hg[:, L * 4 + 3:L * 4 + 4], in1=rinv[:, :],
                            op0=ALU.mult, op1=ALU.mult), reads=[pm, rinv, C.hg], writes=[kmT])
                else:
                    for s_ in range(2):
                        pm = pmk[pmi % 3]
                        pmi += 1
                        for kc in range(16):
                            mm(P, pm[:, 0:256], hTm[:, kc, s_ * 128:(s_ + 1) * 128], wt[:, kc, :], kc == 0, kc == 15,
                               [hTm, wt], [pm], kc == 15)
                        h0 = (blk - 2) * 2
                        P.op("dve", lambda e, pm=pm, s_=s_, h0=h0: e.tensor_copy(
                            out=vme[:, s_, h0:h0 + 2, 0:128], in_=pm[:, 0:256].rearrange("p (h d) -> p h d", h=2)),
                            reads=[pm], writes=[vme])
        P.barrier()
        pms = [P.psum(f"X_pm{i}", [128, T], F32, es) for i in range(4)]
        pos = [P.psum(f"X_po{i}", [128, 129], F32, es) for i in range(4)]
        m_view = C.m_s[L].t.ap().rearrange("(h p) t -> p h t", p=128)

        def load_q(ib):
            q = qm[ib % 2]
            t0 = blocks[ib] * T
            P.dma("sp", lambda e: e.dma_start(out=q[:, :, :], in_=m_view[:, :, t0:t0 + T]),
                  owner=q, reads=[C.m_s[L]], writes=[q])

        units = []
        for ib, tb in enumerate(blocks):
            t0 = tb * T
            q = qm[ib % 2]
            yt = ynt[ib % 2]
            for h in range(4):
                u = len(units)
                pp = [pms[(u % 2) * 2], pms[(u % 2) * 2 + 1]]
                pt = pT[u % 3]
                fzs = [FZ[(u % 2) * 4 + sub] for sub in range(4)]
                pre = (ib + 1) if (h == 0 and ib + 1 < len(blocks)) else None

                def s1(pre=pre, pp=pp, q=q, h=h):
                    if pre is not None:
                        load_q(pre)
                    for kt in range(2):
                        mm(P, pp[kt][:, :], kmT[:, h, kt * 128:(kt + 1) * 128], q[:, h, :], True, True, [kmT, q], [pp[kt]], True)

                def s2(pp=pp, pt=pt):
                    for kt in range(2):
                        act(P, pt[:, kt, :], pp[kt][:, :], AF.Exp, [pp[kt]], [pt])

                def s3(pt=pt, fzs=fzs, h=h):
                    for sub in range(4):
                        po = pos[sub]
                        for kt in range(2):
                            mm(P, po[:, :], pt[:, kt, sub * 128:(sub + 1) * 128], vme[:, kt, h, :], kt == 0, kt == 1,
                               [pt, vme], [po], kt == 1)
                        fin_a(P, C, fzs[sub], po, po, True)

                def s4(fzs=fzs, h=h, yt=yt, t0=t0):
                    c0 = 1536 + h * 128
                    for sub in range(4):
                        fin_b(P, C, fzs[sub], pos[sub], pos[sub], True, gain_o[:, c0:c0 + 128], gain_o,
                              yt[:, sub, h * 128:(h + 1) * 128], yt)
                    if h == 3:
                        P.dma("act", lambda e: e.dma_start(
                            out=C.yn_s[L].t.ap()[t0:t0 + T, 1536:2048].rearrange("(s p) n -> p s n", p=128), in_=yt[:, :, :]),
                            owner=yt, reads=[yt], writes=[C.yn_s[L]])
                units.append([s1, s2, s3, s4])
        load_q(0)
        run_pipeline(units, 4)
    P.barrier()


SPECIAL = {0: 1, 1: 2, 30: 3, 31: 4, 32: 5, 33: 6, 62: 7, 63: 8}


def na_slots(bp):
    if bp in (0, 32):
        return list(range(-2, 4))
    if bp in (31, 63):
        return list(range(-3, 3))
    return list(range(-2, 3))


def alloc_finalize_sets(P, es, pfx, n):
    return [alloc_finalize(P, es, f"{pfx}{i}") for i in range(n)]


def phase_N(P, C, L, blocks):
    with ExitStack() as es:
        FZ = alloc_finalize_sets(P, es, "N", 3)
        gain_o = load_gain(P, C, es, "No", L * 4 + 2)
        tz = P.sbuf("N_tz", [128, 7, 8, 128], BF16, es)
        tzs = [P.sbuf(f"N_tzs{i}", [128, 8, 128], F32, es) for i in range(2)]
        mk = P.sbuf("N_mk", [128, 63, 128], BF16, es)
        P.dma("sp", lambda e: e.dma_start(out=mk[:, :, :], in_=C.mk_d.t.ap()), owner=mk, writes=[mk])
        for d in range(7):
            st = tzs[d % 2]
            P.dma("sp", lambda e, st=st, d=d: e.dma_start(out=st[:, :, :], in_=C.tz_d.t.ap()[L, :, d, :, :]),
                  owner=st, writes=[st])
            act(P, tz[:, d, :, :], st[:, :, :], AF.Copy, [st], [tz])
        tzg = P.sbuf("N_tzg", [128, 5, 8, 128], BF16, es)
        for d in range(5):
            for h in range(8):
                P.op("dve", lambda e, d=d, h=h: e.tensor_tensor(out=tzg[:, d, h, :], in0=tz[:, d + 1, h, :],
                                                                 in1=mk[:, d + 1, :], op=ALU.add),
                     reads=[tz, mk], writes=[tzg])
        Q = [P.sbuf(f"N_q{i}", [128, 8, T], BF16, es) for i in range(2)]
        Kt = [P.sbuf(f"N_k{i}", [128, 8, 1024], BF16, es) for i in range(2)]
        V = [[P.sbuf(f"N_v{i}_{j}", [128, 8, 129], BF16, es) for j in range(8)] for i in range(2)]
        pT = [P.sbuf(f"N_pT{i}", [128, 8, 128], BF16, es) for i in range(3)]
        ynt = [P.sbuf(f"N_ynt{i}", [128, 4, 1024], BF16, es) for i in range(2)]
        psc = [P.psum(f"N_psc{i}", [128, 8, 128], F32, es) for i in range(3)]
        pos = [P.psum(f"N_po{i}", [128, 129], F32, es) for i in range(2)]
        for vv in V:
            for v in vv:
                P.op("dve", lambda e, v=v: e.memset(v[:, :, :], 1.0), writes=[v])
        q_view = C.q_s[L].t.ap().rearrange("(h p) t -> p h t", p=128)
        k_view = C.k_s[L].t.ap().rearrange("(h p) t -> p h t", p=128)
        v_view = C.v_s[L].t.ap().rearrange("(j p) (h d) -> p j h d", p=128, d=128)

        def loads(ib):
            tb = blocks[ib]
            t0 = tb * T
            q, kt, vv = Q[ib % 2], Kt[ib % 2], V[ib % 2]
            P.dma("sp", lambda e: e.dma_start(out=q[:, :, :], in_=q_view[:, :, t0:t0 + T]),
                  owner=q, reads=[C.q_s[L]], writes=[q])
            p_lo = tb * 4 - 2
            j = 0
            while j < 8:
                p = (p_lo + j) % 64
                n = min(8 - j, 64 - p)
                P.dma("sp", lambda e, j=j, p=p, n=n: e.dma_start(
                    out=kt[:, :, j * 128:(j + n) * 128], in_=k_view[:, :, p * 128:(p + n) * 128]),
                    owner=kt, reads=[C.k_s[L]], writes=[kt])
                for jj in range(n):
                    v = vv[j + jj]
                    P.dma("sp", lambda e, v=v, p=p, jj=jj: e.dma_start(out=v[:, :, 0:128], in_=v_view[:, p + jj, :, :]),
                          owner=v, reads=[C.v_s[L]], writes=[v])
                j += n

        units = []
        for ib, tb in enumerate(blocks):
            t0 = tb * T
            q, kt, vv = Q[ib % 2], Kt[ib % 2], V[ib % 2]
            yt = ynt[ib % 2]
            for sb in range(4):
                bp = tb * 4 + sb
                slots = na_slots(bp)
                cls = SPECIAL.get(bp, 0)
                ns = len(slots)
                for h in range(8):
                    u = len(units)
                    pa = psc[u % 3]
                    pt = pT[u % 3]
                    po = pos[u % 2]
                    Fz = FZ[u % 3]
                    pre = (ib + 1) if (sb == 0 and h == 5 and ib + 1 < len(blocks)) else None
                    last = (sb == 3 and h == 7)

                    def s1(pre=pre, pa=pa, kt=kt, q=q, slots=slots, sb=sb, h=h, cls=cls):
                        if pre is not None:
                            loads(pre)
                        for si, dl in enumerate(slots):
                            j = sb + dl + 2
                            pp = pa
                            o_ap = pp[:, si, :]
                            mm(P, o_ap, kt[:, h, j * 128:(j + 1) * 128], q[:, h, sb * 128:(sb + 1) * 128], True, False,
                               [kt, q], [pp], False)
                            fin = (si == len(slots) - 1)
                            if cls == 0:
                                mm(P, o_ap, C.ident[:, :], tzg[:, dl + 2, h, :], False, True, [C.ident, tzg], [pp], fin)
                            else:
                                mm(P, o_ap, C.ident[:, :], tz[:, dl + 3, h, :], False, False, [C.ident, tz], [pp], False)
                                mm(P, o_ap, C.ident[:, :], mk[:, cls * 7 + dl + 3, :], False, True, [C.ident, mk], [pp], fin)

                    def s2(pa=pa, pt=pt, po=po, vv=vv, slots=slots, ns=ns, sb=sb, h=h):
                        act(P, pt[:, 0:ns, :], pa[:, 0:ns, :], AF.Exp, [pa], [pt])
                        for si, dl in enumerate(slots):
                            v = vv[sb + dl + 2]
                            mm(P, po[:, :], pt[:, si, :], v[:, h, :], si == 0, si == ns - 1, [pt, v], [po], si == ns - 1)

                    def s3(Fz=Fz, po=po):
                        fin_a(P, C, Fz, po, po, True)

                    def s4(Fz=Fz, po=po, yt=yt, sb=sb, h=h, last=last, t0=t0):
                        fin_b(P, C, Fz, po, po, True, gain_o[:, h * 128:(h + 1) * 128], gain_o,
                              yt[:, sb, h * 128:(h + 1) * 128], yt)
                        if last:
                            P.dma("act", lambda e: e.dma_start(
                                out=C.yn_s[L].t.ap()[t0:t0 + T, 0:1024].rearrange("(s p) n -> p s n", p=128), in_=yt[:, :, :]),
                                owner=yt, reads=[yt], writes=[C.yn_s[L]])
                    units.append([s1, s2, s3, s4])
        loads(0)
        run_pipeline(units, 4)
    P.barrier()


def phase_F(P, C, L, nk2):
    with ExitStack() as es:
        FZ = alloc_finalize_sets(P, es, "F", 3)
        gain_o = load_gain(P, C, es, "Fo", L * 4 + 2)
        dft = P.sbuf("F_dft", [64, 128], BF16, es)
        ccsc = P.sbuf("F_ccsc", [128, 2, 128], F32, es)
        wf = P.sbuf("F_wf", [128, 4, 128], F32, es)
        ab = P.sbuf("F_ab", [128, 4, 2, 128], BF16, es)
        uA = [P.sbuf(f"F_uA{i}", [64, 32, 128], BF16, es) for i in range(2)]
        Z = P.sbuf("F_Z", [128, 64, 2, 128], BF16, es)
        mbt = [P.sbuf(f"F_mbt{i}", [128, 8, 2, 2, 128], BF16, es) for i in range(2)]
        xT = [P.sbuf(f"F_xT{i}", [128, 2, 128], BF16, es) for i in range(3)]
        ynf = [P.sbuf(f"F_ynf{i}", [128, 8, 128], BF16, es) for i in range(2)]
        pz = [P.psum(f"F_pz{i}", [128, 4, 128], F32, es) for i in range(2)]
        px = [P.psum(f"F_px{i}", [128, 2, 128], F32, es) for i in range(2)]
        py = [P.psum(f"F_py{i}", [128, 128], F32, es) for i in range(4)]
        P.dma("sp", lambda e: e.dma_start(out=dft[:, :], in_=C.dft_d.t.ap()), owner=dft, writes=[dft])
        P.dma("sp", lambda e: e.dma_start(out=ccsc[:, :, :], in_=C.ccsc_d.t.ap()), owner=ccsc, writes=[ccsc])
        P.dma("sp", lambda e: e.dma_start(out=wf[:, :, :], in_=C.wf_d.t.ap()[L].rearrange("g c e -> c g e")),
              owner=wf, writes=[wf])
        for g in range(4):
            for w in range(2):
                pp = py[(g * 2 + w) % 4]
                mm(P, pp[:, :], ccsc[:, w, :], wf[:, g, :], True, True, [ccsc, wf], [pp], True)
                act(P, ab[:, g, w, :], pp[:, :], AF.Copy, [pp], [ab], scale=1.0 / 1024.0)
        u_view = C.u_s[L].t.ap().rearrange("c (n1 n2) -> n1 c n2", n2=128)
        y_view = C.yn_s[L].t.ap().rearrange("(k2 k1) e -> k2 k1 e", k1=64)
        mb_view = C.mb_d.t.ap()
        st = {"iz": 0}

        def stage_A(g):
            for qd in range(4):
                ua = uA[(g * 4 + qd) % 2]
                c0 = g * 128 + qd * 32
                P.dma("sp", lambda e, ua=ua, c0=c0: e.dma_start(out=ua[:, :, :], in_=u_view[:, c0:c0 + 32, :]),
                      owner=ua, reads=[C.u_s[L]], writes=[ua])
                for c4 in range(8):
                    pp = pz[st["iz"] % 2]
                    st["iz"] += 1
                    for i in range(4):
                        c = c4 * 4 + i
                        mm(P, pp[:, i, :], ua[:, c, :], dft[:, :], True, True, [ua, dft], [pp], i == 3)
                    cz = qd * 32 + c4 * 4
                    dst = Z[:, :, :, cz:cz + 4].rearrange("p k r c -> p c r k")
                    src = pp[:, :, :].rearrange("p c (r k) -> p c r k", r=2)
                    if c4 % 2 == 0:
                        act(P, dst, src, AF.Copy, [pp], [Z])
                    else:
                        P.op("dve", lambda e, dst=dst, src=src: e.tensor_copy(out=dst, in_=src), reads=[pp], writes=[Z])

        def load_mb(ch):
            mt = mbt[ch % 2]
            kc8 = ch % 8
            P.dma("sp", lambda e: e.dma_start(out=mt[:, :, :, :, :], in_=mb_view[:, kc8 * 8:(kc8 + 1) * 8, :, :, :]),
                  owner=mt, reads=[C.mb_d], writes=[mt])

        units = []
        for g in range(4):
            for kc8 in range(8):
                ch = g * 8 + kc8
                mt = mbt[ch % 2]
                yf = ynf[ch % 2]
                for i in range(8):
                    u = len(units)
                    k1 = kc8 * 8 + i
                    pxx = px[u % 2]
                    xt = xT[u % 3]
                    pyy = py[u % 4]
                    Fz = FZ[u % 3]
                    c0 = 1024 + g * 128

                    def s1(g=g, kc8=kc8, i=i, ch=ch, mt=mt, pxx=pxx, k1=k1):
                        if kc8 == 0 and i == 0:
                            stage_A(g)
                        if i == 0 and ch + 1 < 32:
                            load_mb(ch + 1)
                        mm(P, pxx[:, :, 0:nk2], Z[:, k1, 0, :], mt[:, i, 0, :, 0:nk2], True, False, [Z, mt], [pxx], False)
                        mm(P, pxx[:, :, 0:nk2], Z[:, k1, 1, :], mt[:, i, 1, :, 0:nk2], False, True, [Z, mt], [pxx], True)

                    def s2(g=g, i=i, pxx=pxx, xt=xt, pyy=pyy):
                        act(P, xt[:, :, 0:nk2], pxx[:, :, 0:nk2], AF.Copy, [pxx], [xt])
                        mm(P, pyy[0:nk2, :], xt[:, 0, 0:nk2], ab[:, g, 0, :], True, False, [xt, ab], [pyy], False)
                        mm(P, pyy[0:nk2, :], xt[:, 1, 0:nk2], ab[:, g, 1, :], False, True, [xt, ab], [pyy], True)

                    def s3(Fz=Fz, pyy=pyy, i=i):
                        fin_a(P, C, Fz, pyy, pyy, "copy", n=nk2)

                    def s4(Fz=Fz, pyy=pyy, i=i, yf=yf, c0=c0, kc8=kc8):
                        fin_b(P, C, Fz, pyy, pyy, "copy", gain_o[0:nk2, c0:c0 + 128], gain_o,
                              yf[0:nk2, i, :], yf, n=nk2)
                        if i == 7:
                            P.dma("act", lambda e: e.dma_start(
                                out=y_view[0:nk2, kc8 * 8:(kc8 + 1) * 8, c0:c0 + 128], in_=yf[0:nk2, :, :]),
                                owner=yf, reads=[yf], writes=[C.yn_s[L]])
                    units.append([s1, s2, s3, s4])
        load_mb(0)
        run_pipeline(units, 4)
    P.barrier()


def phase_O(P, C, L, tiles, x_src, x_dst):
    with ExitStack() as es:
        gain_f = load_gain(P, C, es, "Of", L * 4 + 3)
        hn = [P.sbuf(f"O_hn{i}", [128, D], BF16, es) for i in range(2)]
        B = Ctx()
        B.tp = [P.psum(f"O_tp{i}", [128, 8, 128], BF16, es) for i in range(2)]
        ssq = P.sbuf("O_ssq", [128, 1], F32, es)
        sd = P.sbuf("O_sd", [128, 1], F32, es)
        rs = P.sbuf("O_rs", [128, 1], F32, es)
        hT = P.sbuf("O_hT", [128, 16, T], BF16, es)
        xms = [P.sbuf(f"O_xm{i}", [128, D], F32, es) for i in range(4)]
        aT = P.sbuf("O_aT", [128, NFF, T], BF16, es)
        wbig = [P.sbuf(f"O_wbig{i}", [128, 22, 512], BF16, es) for i in range(2)]
        wgu = [P.sbuf(f"O_wgu{i}", [128, 16, 256], BF16, es) for i in range(4)]
        sg = [P.sbuf(f"O_sg{i}", [128, T], F32, es) for i in range(2)]
        pa = [P.psum(f"O_pa{i}", [128, T], F32, es) for i in range(6)]
        wo_view = C.wbf[("w_out", L)].t.ap().rearrange("(kc p) n -> p kc n", p=128)
        wg_view = C.wbf[("w_gate", L)].t.ap().rearrange("(kc p) n -> p kc n", p=128)
        wu_view = C.wbf[("w_up", L)].t.ap().rearrange("(kc p) n -> p kc n", p=128)
        wd_view = C.wbf[("w_down", L)].t.ap().rearrange("(fc p) n -> p fc n", p=128)
        ipa = 0
        ibig = 0
        igu = 0
        for ti in tiles:
            t0 = ti * T
            for s in range(4):
                h = hn[s % 2]
                P.dma("sp", lambda e, h=h, s=s, t0=t0: e.dma_start(
                    out=h[:, :], in_=C.yn_s[L].t.ap()[t0 + s * 128:t0 + (s + 1) * 128, :]),
                    owner=h, reads=[C.yn_s[L]], writes=[h])
                transpose_rows(P, C, B, h, hT, s)
            for s_ in range(4):
                xm = xms[s_]
                P.dma("sp", lambda e, t0=t0, xm=xm, s_=s_: e.dma_start(
                    out=xm[:, :], in_=x_src.t.ap()[t0 + s_ * 128:t0 + (s_ + 1) * 128, :]),
                    owner=xm, reads=[x_src], writes=[xm])
            for cb in range(4):
                wb = wbig[ibig % 2]
                ibig += 1
                P.dma("sp", lambda e, wb=wb, cb=cb: e.dma_start(out=wb[:, 0:16, :], in_=wo_view[:, :, cb * 512:(cb + 1) * 512]),
                      owner=wb, reads=[C.wbf[("w_out", L)]], writes=[wb])
                for s in range(4):
                    pp = pa[ipa % 6]
                    ipa += 1
                    for kc in range(16):
                        mm(P, pp[:, :], hT[:, kc, s * 128:(s + 1) * 128], wb[:, kc, :], kc == 0, kc == 15, [hT, wb], [pp], kc == 15)
                    xm = xms[s]
                    P.op("dve", lambda e, pp=pp, xm=xm, cb=cb: e.tensor_tensor(
                        out=xm[:, cb * 512:(cb + 1) * 512], in0=pp[:, :], in1=xm[:, cb * 512:(cb + 1) * 512], op=ALU.add),
                        reads=[pp, xm], writes=[xm])
            for s in range(4):
                h = hn[s % 2]
                xm = xms[s]
                act(P, h[:, :], xm[:, :], AF.Square, [xm], [h, ssq], accum=ssq[:, 0:1])
                act(P, sd[:, 0:1], ssq[:, 0:1], AF.Ln, [ssq, C.epsb], [sd], scale=1.0 / D, bias=C.epsb[:, 0:1])
                act(P, rs[:, 0:1], sd[:, 0:1], AF.Exp, [sd], [rs], scale=-0.5)
                P.op("dve", lambda e, h=h, xm=xm: e.scalar_tensor_tensor(
                    out=h[:, :], in0=xm[:, :], scalar=rs[:, 0:1], in1=gain_f[:, :], op0=ALU.mult, op1=ALU.mult),
                    reads=[xm, rs, gain_f], writes=[h])
                transpose_rows(P, C, B, h, hT, s)
            for fb in range(NFF // 2):
                wg = wgu[igu % 4]
                wu = wgu[(igu + 1) % 4]
                igu += 2
                P.dma("sp", lambda e, wg=wg, fb=fb: e.dma_start(out=wg[:, :, :], in_=wg_view[:, :, fb * 256:(fb + 1) * 256]),
                      owner=wg, reads=[C.wbf[("w_gate", L)]], writes=[wg])
                P.dma("sp", lambda e, wu=wu, fb=fb: e.dma_start(out=wu[:, :, :], in_=wu_view[:, :, fb * 256:(fb + 1) * 256]),
                      owner=wu, reads=[C.wbf[("w_up", L)]], writes=[wu])
                for half in range(2):
                    fc = fb * 2 + half
                    pg = pa[ipa % 6]
                    pu = pa[(ipa + 1) % 6]
                    ipa += 2
                    for kc in range(16):
                        mm(P, pg[:, :], wg[:, kc, half * 128:(half + 1) * 128], hT[:, kc, :], kc == 0, kc == 15, [hT, wg], [pg], kc == 15)
                    for kc in range(16):
                        mm(P, pu[:, :], wu[:, kc, half * 128:(half + 1) * 128], hT[:, kc, :], kc == 0, kc == 15, [hT, wu], [pu], kc == 15)
                    sgb = sg[fc % 2]
                    act(P, sgb[:, :], pg[:, :], AF.Silu, [pg], [sgb])
                    P.op("dve", lambda e, sgb=sgb, pu=pu, fc=fc: e.tensor_tensor(
                        out=aT[:, fc, :], in0=sgb[:, :], in1=pu[:, :], op=ALU.mult), reads=[sgb, pu], writes=[aT])
            for cb in range(4):
                for half in range(2):
                    wb = wbig[ibig % 2]
                    ibig += 1
                    P.dma("sp", lambda e, wb=wb, cb=cb, half=half: e.dma_start(
                        out=wb[:, :, :], in_=wd_view[:, half * 22:(half + 1) * 22, cb * 512:(cb + 1) * 512]),
                        owner=wb, reads=[C.wbf[("w_down", L)]], writes=[wb])
                    for s in range(4):
                        pp = pa[ipa % 6]
                        ipa += 1
                        for j in range(22):
                            mm(P, pp[:, :], aT[:, half * 22 + j, s * 128:(s + 1) * 128], wb[:, j, :], j == 0, j == 21, [aT, wb], [pp], j == 21)
                        xm = xms[s]
                        P.op("dve", lambda e, pp=pp, xm=xm, cb=cb: e.tensor_tensor(
                            out=xm[:, cb * 512:(cb + 1) * 512], in0=pp[:, :], in1=xm[:, cb * 512:(cb + 1) * 512], op=ALU.add),
                            reads=[pp, xm], writes=[xm])
            for s_ in range(4):
                xm = xms[s_]
                P.dma("act", lambda e, t0=t0, xm=xm, s_=s_: e.dma_start(
                    out=x_dst.t.ap()[t0 + s_ * 128:t0 + (s_ + 1) * 128, :], in_=xm[:, :]),
                    owner=xm, reads=[xm], writes=[x_dst])
    P.barrier()


WEIGHTS = (("w_in", D, DIN), ("w_mem_kv", D, 1024), ("w_out", D, D), ("w_gate", D, DFF), ("w_up", D, DFF),
           ("w_down", DFF, D))


def build(cfg):
    nc = bass.Bass("TRN2", target_bir_lowering=False)
    dbg = cfg.get("debug", ())

    def ein(name, shape, dt=F32):
        return Buf(name, nc.dram_tensor(name, list(shape), dt, kind="ExternalInput"))

    with ExitStack() as es:
        P = Prog(nc, es, same_engine_sync=cfg.get("same", True))
        C = Ctx()
        C.x = ein("x", [S, D])
        C.mem = ein("mem", [256, D])
        C.win = {name: ein(name, [2, r, c]) for name, r, c in WEIGHTS}
        C.gb_d = ein("gb", [128, 8, D])
        C.hg_d = ein("hg", [128, 8])
        C.ident_d = ein("ident", [128, 128], BF16)

        def scratch(name, shape, dt):
            kind = "ExternalOutput" if name in dbg else "Internal"
            return Buf(name, nc.dram_tensor(name, list(shape), dt, kind=kind))

        C.wbf = {(name, L): scratch(f"{name}_bf{L}", [r, c], BF16) for name, r, c in WEIGHTS for L in range(2)}
        C.q_s = [scratch(f"q_s{L}", [1024, S], BF16) for L in range(2)]
        C.k_s = [scratch(f"k_s{L}", [1024, S], BF16) for L in range(2)]
        C.v_s = [scratch(f"v_s{L}", [S, 1024], BF16) for L in range(2)]
        C.u_s = [scratch(f"u_s{L}", [512, S], BF16) for L in range(2)]
        C.m_s = [scratch(f"m_s{L}", [512, S], BF16) for L in range(2)]
        C.yn_s = [scratch(f"yn_s{L}", [S, D], BF16) for L in range(2)]
        C.x1 = scratch("x1_s", [S, D], F32)
        C.out = Buf("out", nc.dram_tensor("out", [S // 2, D], F32, kind="ExternalOutput"))
        C.tz_d = ein("tz", [2, 128, 7, 8, 128])
        C.mk_d = ein("mk", [128, 63, 128], BF16)
        C.dft_d = ein("dft", [64, 128], BF16)
        C.mb_d = ein("mb", [128, 64, 2, 2, 128], BF16)
        C.ccsc_d = ein("ccsc", [128, 2, 128])
        C.wf_d = ein("wf", [2, 4, 128, 128])

        C.hg = P.sbuf("hg_sb", [128, 8], F32)
        C.ident = P.sbuf("ident_sb", [128, 128], BF16)
        C.ones = P.sbuf("ones_sb", [128, 128], BF16)
        C.epsb = P.sbuf("epsb_sb", [128, 1], F32)
        P.dma("sp", lambda e: e.dma_start(out=C.hg[:, :], in_=C.hg_d.t.ap()), owner=C.hg, writes=[C.hg])
        P.dma("sp", lambda e: e.dma_start(out=C.ident[:, :], in_=C.ident_d.t.ap()), owner=C.ident, writes=[C.ident])
        P.op("dve", lambda e: e.memset(C.ones[:, :], 1.0), writes=[C.ones])
        P.op("dve", lambda e: e.memset(C.epsb[:, :], EPS), writes=[C.epsb])
        for col in (0, 2, 4, 6):
            P.op("dve", lambda e, col=col: e.tensor_scalar(out=C.hg[:, col:col + 1], in0=C.hg[:, col:col + 1],
                                                            scalar1=128.0 ** -0.5, scalar2=None, op0=ALU.mult),
                 reads=[C.hg], writes=[C.hg])

        cvt_owner = Buf("cvt")
        P._dsem(cvt_owner)
        P.bg_sems = {cvt_owner.dsem}
        wanted = cfg.get("weights", [(n, L) for L in range(2) for n, _, _ in WEIGHTS])

        def convert(names, L_sel):
            for (name, L) in wanted:
                if L != L_sel or name not in names:
                    continue
                rows = dict((n, r) for n, r, c in WEIGHTS)[name]
                src = C.win[name]
                dst = C.wbf[(name, L)]
                for r0 in range(0, rows, 256):
                    P.dma("pool", lambda e, src=src, dst=dst, L=L, r0=r0: e.dma_start(
                        out=dst.t.ap()[r0:r0 + 256, :], in_=src.t.ap()[L, r0:r0 + 256, :]),
                        owner=cvt_owner, reads=[src], writes=[dst])
        convert(("w_in", "w_mem_kv", "w_out"), 0)
        stage = {"n": 0}

        for ph in cfg["phases"]:
            if ph[0] == "A":
                _, L, tiles = ph
                phase_A(P, C, L, C.x if L == 0 else C.x1, tiles)
                if stage["n"] == 0:
                    convert(("w_gate", "w_up", "w_down"), 0)
                    convert(tuple(n for n, _, _ in WEIGHTS), 1)
                    stage["n"] = 2
            elif ph[0] == "X":
                phase_X(P, C, ph[1], ph[2])
            elif ph[0] == "N":
                phase_N(P, C, ph[1], ph[2])
            elif ph[0] == "F":
                phase_F(P, C, ph[1], ph[2])
                if stage["n"] == 1:
                    convert(tuple(n for n, _, _ in WEIGHTS), 1)
                    stage["n"] = 2
            elif ph[0] == "O":
                L = ph[1]
                phase_O(P, C, L, ph[2], C.x if L == 0 else C.x1, C.x1 if L == 0 else C.out)

        if stage["n"] < 2:
            if stage["n"] == 0:
                convert(("w_gate", "w_up", "w_down"), 0)
            convert(tuple(n for n, _, _ in WEIGHTS), 1)
        P.barrier(final=True)
        stuck = P.check()
        if stuck:
            raise RuntimeError(f"sync graph deadlock: {stuck}")
        P.emit()
    return nc


def host_inputs(inputs, core):
    b, hf = core // 2, core % 2
    f32 = np.float32
    x = np.asarray(inputs["x"][b])
    if hf:
        x = np.concatenate([x[S // 2:], x[:S // 2]], 0)
    m = {"x": np.ascontiguousarray(x, dtype=f32), "mem": np.ascontiguousarray(inputs["mem"][b], dtype=f32)}
    for name, _, _ in WEIGHTS:
        m[name] = np.ascontiguousarray(inputs[name], dtype=f32)
    gb = np.stack([inputs[k][L] for L in range(2) for k in ("attn_norm", "mem_norm", "out_norm", "ffn_norm")], 0)
    m["gb"] = np.ascontiguousarray(np.broadcast_to(gb[None], (128, 8, D)), dtype=f32)
    hg = np.stack([inputs[k][L] for L in range(2) for k in ("na_q_norm", "na_k_norm", "mem_q_norm", "mem_k_norm")], 1)
    m["hg"] = np.ascontiguousarray(hg, dtype=f32)
    m["ident"] = np.eye(128, dtype=f32).astype(ml_dtypes.bfloat16)
    return m


def host_tables(inputs, hf):
    f32 = np.float32
    bf = ml_dtypes.bfloat16
    rpb = np.asarray(inputs["na_rpb"], dtype=f32)
    kc = np.arange(64)[:, None]
    qc = np.arange(64)[None, :]
    cstart = np.clip(qc - 8, 0, 48)
    colmask = (kc >= cstart) & (kc < cstart + 16)
    colidx = np.clip(kc - qc + 15, 0, 30)
    tz = np.full((2, 128, 7, 8, 128), NEG, dtype=f32)
    for L in range(2):
        for dl in range(-3, 4):
            for a in range(2):
                for r in range(2):
                    d = 2 * dl + a - r
                    if abs(d) > 7:
                        continue
                    blk = rpb[L][:, d + 7, :][:, colidx]
                    blk = np.where(colmask[None], blk, f32(NEG))
                    tz[L, a * 64:(a + 1) * 64, dl + 3, :, r * 64:(r + 1) * 64] = blk.transpose(1, 0, 2)
    mk = np.full((128, 63, 128), NEG, dtype=f32)
    inv = {v: k for k, v in SPECIAL.items()}
    for cls in range(9):
        for dl in range(-3, 4):
            for a in range(2):
                for r in range(2):
                    if cls == 0:
                        d = 2 * dl + a - r
                        ok = -4 <= d <= 3
                    else:
                        bp = inv[cls]
                        rq = (2 * bp + r + 64 * hf) % 128
                        rk = (2 * ((bp + dl) % 64) + a + 64 * hf) % 128
                        st = min(max(rq - 4, 0), 120)
                        ok = st <= rk <= st + 7
                    if ok:
                        mk[a * 64:(a + 1) * 64, cls * 7 + dl + 3, r * 64:(r + 1) * 64] = 0.0
    n1 = np.arange(64, dtype=np.float64)[:, None]
    k1 = np.arange(64, dtype=np.float64)[None, :]
    ang = 2 * np.pi * n1 * k1 / 64
    dft = np.concatenate([np.cos(ang), -np.sin(ang)], 1)
    n2 = np.arange(128, dtype=np.float64)[:, None, None]
    k1 = np.arange(64, dtype=np.float64)[None, :, None]
    k2 = np.arange(128, dtype=np.float64)[None, None, :]
    ang = 2 * np.pi * n2 * (k1 + 64 * k2) / S
    sign = np.where(((n2 + k1) % 2) == 1, -1.0, 1.0) if hf else 1.0
    Mr = np.cos(ang) * sign
    Mi = -np.sin(ang) * sign
    mb = np.stack([np.stack([Mr, Mi], 2), np.stack([-Mi, Mr], 2)], 2)
    c = np.arange(128, dtype=np.float64)
    angc = 2 * np.pi * np.outer(c, c) / 128
    ccsc = np.stack([np.cos(angc), np.sin(angc)], 1)
    return {"tz": tz, "mk": mk.astype(bf), "dft": dft.astype(f32).astype(bf), "mb": mb.astype(f32).astype(bf),
            "ccsc": ccsc.astype(f32), "wf": np.ascontiguousarray(inputs["w_fourier"], dtype=f32)}


ALL = "qkvum"
FULL_CFG = {
    "phases": [
        ("A", 0, [(i, ALL) for i in range(16)]),
        ("X", 0, list(range(16))),
        ("N", 0, list(range(16))),
        ("F", 0, 128),
        ("O", 0, list(range(16))),
        ("A", 1, [(i, ALL) for i in range(8)] + [(8, "kvu")] + [(i, "u") for i in range(9, 15)] + [(15, "kvu")]),
        ("X", 1, list(range(8))),
        ("N", 1, list(range(8))),
        ("F", 1, 64),
        ("O", 1, list(range(8))),
    ],
}
_NC = {}


def kernel(**inputs):
    inputs = {k: np.asarray(v) for k, v in inputs.items()}
    if "nc" not in _NC:
        _NC["nc"] = build(FULL_CFG)
    tabs = [host_tables(inputs, hf) for hf in range(2)]
    in_maps = []
    for core in range(8):
        m = host_inputs(inputs, core)
        m.update(tabs[core % 2])
        in_maps.append(m)
    res = run_bass_kernel_spmd(_NC["nc"], in_maps, core_ids=list(range(8)))
    out = np.empty((4, S, D), dtype=np.float32)
    for core in range(8):
        b, hf = core // 2, core % 2
        out[b, hf * (S // 2):(hf + 1) * (S // 2)] = res.results[core]["out"]
    return out
```
